# Optimizing a Trainium2 kernel written in Bass

```python
import jax, jax.numpy as jnp
from jax import lax
import numpy as np

D_MODEL = 1024
BATCH = 4
SEQ = 8192
DEPTH = 1
DEC_BATCH = 16
DEC_SEQ = 4096
PAST_LEN = 128

D_MIX = D_MODEL
D_FOURIER = D_MIX // 2
N_F_HEADS = 4
F_HEAD_DIM = D_FOURIER // N_F_HEADS
D_CONV = D_MIX - D_FOURIER
N_C_GROUPS = 4
CONV_WIDTH = 3
D_IN = D_FOURIER + 3 * D_CONV
D_FF = 2816
RMS_EPS = 1e-6
FFN_RES_SCALE = 0.5

kernel_name = "hybrid_fnet_shortconv_macaron_encoder"


def rms_norm(x, g):
    xf = x.astype(jnp.float32)
    y = xf * lax.rsqrt(jnp.mean(xf * xf, axis=-1, keepdims=True) + RMS_EPS)
    return (y * g.astype(jnp.float32)).astype(x.dtype)


def swiglu(h, w_gate, w_up, w_down):
    return (jax.nn.silu(h @ w_gate) * (h @ w_up)) @ w_down


def fourier_mix(u):
    b, s, _ = u.shape
    uh = u.astype(jnp.float32).reshape(b, s, N_F_HEADS, F_HEAD_DIM)
    y = jnp.fft.fft2(uh, axes=(1, 3), norm="ortho").real
    return y.reshape(b, s, D_FOURIER).astype(u.dtype)


def centred_short_conv(v, w):
    vp = jnp.pad(v, ((0, 0), (1, 1), (0, 0)))
    return vp[:, :-2] * w[0] + vp[:, 1:-1] * w[1] + vp[:, 2:] * w[2]


def encoder_layer(x, g_ffn1, w1_gate, w1_up, w1_down, g_mix, w_in, conv_w,
                  g_fourier, g_conv, w_out, g_ffn2, w2_gate, w2_up, w2_down):
    x = x + FFN_RES_SCALE * swiglu(rms_norm(x, g_ffn1), w1_gate, w1_up, w1_down)
    h = rms_norm(x, g_mix)
    u = h @ w_in
    u_f = u[..., :D_FOURIER]
    gate_b, gate_c, v = jnp.split(u[..., D_FOURIER:], 3, axis=-1)
    y_f = rms_norm(fourier_mix(u_f), g_fourier)
    y_c = rms_norm(gate_b * centred_short_conv(gate_c * v, conv_w), g_conv)
    x = x + jnp.concatenate([y_f, y_c], axis=-1) @ w_out
    x = x + FFN_RES_SCALE * swiglu(rms_norm(x, g_ffn2), w2_gate, w2_up, w2_down)
    return x


def setup_inputs(seed: int = 0) -> dict:
    key = jax.random.key(seed)
    ks = jax.random.split(key, 20)
    f32 = jnp.float32

    def nrm(k, shape, scale):
        return jax.random.normal(k, shape, f32) * scale

    def gain(k, n):
        return 1.0 + 0.05 * jax.random.normal(k, (DEPTH, n), f32)

    return {
        "x_prompt": jax.random.normal(ks[0], (BATCH, SEQ, D_MODEL), f32),
        "x_sample": jax.random.normal(ks[1], (DEC_BATCH, DEC_SEQ, D_MODEL), f32),
        "g_ffn1": gain(ks[2], D_MODEL),
        "w1_gate": nrm(ks[3], (DEPTH, D_MODEL, D_FF), D_MODEL ** -0.5),
        "w1_up": nrm(ks[4], (DEPTH, D_MODEL, D_FF), D_MODEL ** -0.5),
        "w1_down": nrm(ks[5], (DEPTH, D_FF, D_MODEL), D_FF ** -0.5),
        "g_mix": gain(ks[6], D_MODEL),
        "w_in": nrm(ks[7], (DEPTH, D_MODEL, D_IN), D_MODEL ** -0.5),
        "conv_w": nrm(ks[8], (DEPTH, CONV_WIDTH, D_CONV), CONV_WIDTH ** -0.5),
        "g_fourier": gain(ks[9], D_FOURIER),
        "g_conv": gain(ks[10], D_CONV),
        "w_out": nrm(ks[11], (DEPTH, D_MIX, D_MODEL), D_MIX ** -0.5),
        "g_ffn2": gain(ks[12], D_MODEL),
        "w2_gate": nrm(ks[13], (DEPTH, D_MODEL, D_FF), D_MODEL ** -0.5),
        "w2_up": nrm(ks[14], (DEPTH, D_MODEL, D_FF), D_MODEL ** -0.5),
        "w2_down": nrm(ks[15], (DEPTH, D_FF, D_MODEL), D_FF ** -0.5),
        "g_final": 1.0 + 0.05 * jax.random.normal(ks[16], (D_MODEL,), f32),
    }


def reference(x_prompt, x_sample, g_ffn1, w1_gate, w1_up, w1_down, g_mix, w_in,
              conv_w, g_fourier, g_conv, w_out, g_ffn2, w2_gate, w2_up, w2_down,
              g_final):
    def trunk(x):
        for l in range(DEPTH):
            x = encoder_layer(x, g_ffn1[l], w1_gate[l], w1_up[l], w1_down[l],
                              g_mix[l], w_in[l], conv_w[l], g_fourier[l], g_conv[l],
                              w_out[l], g_ffn2[l], w2_gate[l], w2_up[l], w2_down[l])
        return rms_norm(x, g_final)

    y_prompt = trunk(x_prompt)
    y_sample = trunk(x_sample)
    return (y_prompt, y_sample)
```

```python
import numpy as np
import ml_dtypes
from contextlib import ExitStack

import concourse.bass as bass
import concourse.mybir as mybir
from concourse.bass_utils import run_bass_kernel_spmd

F32 = mybir.dt.float32
BF16 = mybir.dt.bfloat16
AF = mybir.ActivationFunctionType
ALU = mybir.AluOpType

D = 1024
DFF = 2816
NJ = DFF // 128
NU = NJ // 2
NTOK = 12288
EPS = 1e-6
N_CORES = 8


class Op:
    __slots__ = ("eng", "fn", "reads", "writes", "dma", "idx", "eidx", "signal", "sem",
                 "semval", "waits", "presem", "group")

    def __init__(self, eng, fn, reads, writes, dma):
        self.eng = eng
        self.fn = fn
        self.reads = reads
        self.writes = writes
        self.dma = dma
        self.signal = False
        self.sem = None
        self.semval = 0
        self.waits = []
        self.presem = None


class Prog:
    ENGS = ("pe", "act", "dve", "pool", "sp")
    NSEM_ENG = 4
    NSEM_DMA = 20

    def __init__(self, nc):
        self.nc = nc
        self.ops = []
        self.last_writer = {}
        self.readers = {}
        self.exclusive = set()

    def add(self, eng, fn, reads=(), writes=(), dma=False, group=None):
        ex = [r for r in reads if r in self.exclusive]
        if ex:
            writes = list(writes) + [r for r in ex if r not in writes]
        op = Op(eng, fn, tuple(reads), tuple(writes), dma)
        op.idx = len(self.ops)
        op.group = group
        self.ops.append(op)
        return op

    def _analyze(self):
        deps_of = []
        wgroup = {}
        for op in self.ops:
            deps = set()
            for r in op.reads:
                for w in self.last_writer.get(r, ()):
                    deps.add(w)
            for r in op.writes:
                gi = wgroup.get(r)
                if op.group is not None and gi is not None and gi[0] == op.group and not self.readers.get(r):
                    deps |= gi[1]
                    continue
                for w in self.last_writer.get(r, ()):
                    deps.add(w)
                for rd in self.readers.get(r, ()):
                    deps.add(rd)
            for r in op.reads:
                self.readers.setdefault(r, []).append(op.idx)
            for r in op.writes:
                gi = wgroup.get(r)
                if op.group is not None and gi is not None and gi[0] == op.group and not self.readers.get(r):
                    self.last_writer[r].append(op.idx)
                else:
                    wdeps = set(self.last_writer.get(r, ())) | set(self.readers.get(r, ()))
                    wgroup[r] = (op.group, wdeps)
                    self.last_writer[r] = [op.idx]
                    self.readers[r] = []
            deps.discard(op.idx)
            deps_of.append(deps)
        cnt = {e: 0 for e in self.ENGS}
        for op in self.ops:
            op.eidx = cnt[op.eng]
            cnt[op.eng] += 1
        water = {e: {p: -1 for p in self.ENGS} for e in self.ENGS}
        dma_waited = {e: set() for e in self.ENGS}
        for op in self.ops:
            best = {}
            for d in deps_of[op.idx]:
                p = self.ops[d]
                if p.dma:
                    if d not in dma_waited[op.eng]:
                        dma_waited[op.eng].add(d)
                        op.waits.append(d)
                        p.signal = True
                else:
                    if p.eng == op.eng and not op.dma:
                        if op.eng == "pe":
                            continue
                    if p.eidx > water[op.eng][p.eng]:
                        if p.eng not in best or self.ops[best[p.eng]].eidx < p.eidx:
                            best[p.eng] = d
            for pe_, d in best.items():
                p = self.ops[d]
                water[op.eng][pe_] = p.eidx
                op.waits.append(d)
                p.signal = True

    def _assign_sems(self, es):
        nc = self.nc
        self.eng_sems = {e: [es.enter_context(nc.semaphore(f"s_{e}{i}")) for i in range(self.NSEM_ENG)]
                         for e in ("pe", "act", "dve", "pool")}
        self.dma_sems = {e: [es.enter_context(nc.semaphore(f"d_{e}{i}")) for i in range(self.NSEM_DMA)]
                         for e in ("sp", "act", "pool")}
        ecount = {e: 0 for e in self.ENGS}
        dcount = {e: 0 for e in self.ENGS}
        dma_last = {}
        dma_val = {}
        for op in self.ops:
            if op.dma:
                k = dcount[op.eng]
                dcount[op.eng] += 1
                sem = self.dma_sems[op.eng][k % self.NSEM_DMA]
                key = (op.eng, k % self.NSEM_DMA)
                prev = dma_last.get(key)
                if prev is not None:
                    op.presem = (prev.sem, prev.semval)
                dma_val[key] = dma_val.get(key, 0) + 16
                op.sem = sem
                op.semval = dma_val[key]
                dma_last[key] = op
            elif op.signal:
                k = ecount[op.eng]
                ecount[op.eng] += 1
                op.sem = self.eng_sems[op.eng][k % self.NSEM_ENG]
                op.semval = k // self.NSEM_ENG + 1
        self.n_dma = dict(dcount)
        self.final_dma = [o for o in dma_last.values()]

    def emit(self, es):
        nc = self.nc
        self._analyze()
        self._assign_sems(es)
        block = es.enter_context(nc.Block())
        by_eng = {e: [o for o in self.ops if o.eng == e] for e in self.ENGS}
        ops = self.ops
        final_dma = self.final_dma

        def run(eng_handle, lst, is_last_waiter=False):
            for op in lst:
                if op.presem is not None:
                    eng_handle.wait_ge(op.presem[0], op.presem[1])
                for d in op.waits:
                    p = ops[d]
                    eng_handle.wait_ge(p.sem, p.semval)
                ins = op.fn(eng_handle)
                if op.dma:
                    ins.then_inc(op.sem, 16)
                elif op.signal:
                    ins.then_inc(op.sem, 1)
            if is_last_waiter:
                for o in final_dma:
                    eng_handle.wait_ge(o.sem, o.semval)

        @block.tensor
        def _(e):
            run(e, by_eng["pe"])

        @block.scalar
        def _(e):
            run(e, by_eng["act"])

        @block.vector
        def _(e):
            run(e, by_eng["dve"])

        @block.gpsimd
        def _(e):
            run(e, by_eng["pool"])

        @block.sync
        def _(e):
            run(e, by_eng["sp"], is_last_waiter=True)


def _W(n, e):
    return np.exp(-2j * np.pi * (np.asarray(e) % n) / n)


def _dft_tables(kind):
    T = 32 if kind == "U" else 64
    J = 128 // T
    A1 = np.zeros((128, 128), complex)
    tw = np.zeros((128, T), complex)
    A2 = np.zeros((128, 128), complex)
    ar = np.arange(128)
    if kind in ("P", "U"):
        S = 128 * T
        A1 = _W(128, np.outer(ar, ar)) / np.sqrt(128)
        tw = _W(S, np.outer(ar, np.arange(T)))
        F = _W(T, np.outer(np.arange(T), np.arange(T))) / np.sqrt(T)
        for js in range(J):
            A2[js * T:(js + 1) * T, js * T:(js + 1) * T] = F
    else:
        a64 = np.arange(64)
        F64 = _W(64, np.outer(a64, a64)) / 8.0
        for a in range(2):
            A1[a * 64:(a + 1) * 64, a * 64:(a + 1) * 64] = F64
            tw[a * 64:(a + 1) * 64, :] = _W(4096, np.outer(a64, a64))
        k2 = np.arange(64)
        for a in range(2):
            for jp in range(2):
                k2p = jp + 2 * (k2 % 32)
                blk = _W(64, np.outer(a64, k2p)) / 8.0 * (k2 // 32 == a)[None, :]
                A2[a * 64:(a + 1) * 64, jp * 64:(jp + 1) * 64] = blk
    G = 128 * T
    mp = np.ones(G, np.float32)
    mn = np.ones(G, np.float32)
    mp[0] = 0
    mn[G - 1] = 0
    if kind == "S":
        mp[4096] = 0
        mn[4095] = 0
    p = np.arange(128)[:, None]
    kk = np.arange(T)[None, :]
    n = kk + T * (p // T) + 128 * (p % T)
    dft = np.stack([A1.real, A1.imag, -A1.imag, A2.real, -A2.imag], axis=1)
    twt = np.stack([tw.real, tw.imag], axis=1)
    mk = np.stack([mp[n], mn[n]], axis=1)
    return (np.ascontiguousarray(dft).astype(ml_dtypes.bfloat16),
            np.ascontiguousarray(twt).astype(np.float32),
            np.ascontiguousarray(mk).astype(np.float32))


def _cs_table():
    c = np.arange(128)
    ang = 2 * np.pi * np.outer(c, c) / 128
    return np.concatenate([np.cos(ang), -np.sin(ang)], axis=1).astype(np.float32) / np.float32(np.sqrt(128))


class WStream:
    def __init__(self, P, name, slots, seq):
        self.P = P
        self.name = name
        self.slots = slots
        self.seq = seq
        self.issued = 0
        self.pos = 0

    def next(self, ap=None, key=None):
        P = self.P
        ns = len(self.slots)
        if key is not None:
            assert self.seq[self.pos][1] == key, (self.seq[self.pos][1], key)
        while self.issued < len(self.seq) and self.issued < self.pos + ns:
            i = self.issued
            src, skey = self.seq[i]
            dst = self.slots[i % ns]
            P.add("sp", lambda e, dst=dst, src=src: e.dma_start(out=dst[:], in_=src),
                  reads=[skey], writes=[(self.name, i % ns)], dma=True)
            self.issued += 1
        i = self.pos
        self.pos += 1
        return self.slots[i % ns], (self.name, i % ns)


GROUPS = (
    ("p", 0, 64),
    ("u", 8192, 32),
)


def build_program(schedule=None, dbg=False, pro=("pad", "cast", "fold"), ng=15, nxt=8):
    nc = bass.Bass("TRN2", target_bir_lowering=False)
    din = lambda n, s, d: nc.dram_tensor(n, s, d, kind="ExternalInput")
    dint = lambda n, s, d: nc.dram_tensor(n, s, d, kind=("ExternalOutput" if dbg else "Internal"))

    x_h = din("x", [NTOK, D], F32)
    wgu_h = [din(f"wgu{f}", [NU, 128, 4096], F32) for f in (1, 2)]
    wd_h = [din(f"wd{f}", [2 * NU, 128, 1024], F32) for f in (1, 2)]
    winr_h = din("winr", [4, 128, 4096], F32)
    wout_h = din("wout", [2, 128, 4096], F32)
    csb_h = din("csb", [128, 256], BF16)
    gtab_h = din("gtab", [128, 4 * 1024], F32)
    gfc_h = din("gfc", [128, 2 * 512], F32)
    cw_h = din("cw", [128, 3 * 512], F32)
    ident_h = din("ident", [128, 128], BF16)
    dft_h = {"p": din("dft_p", [128, 5 * 128], BF16), "u": din("dft_u", [128, 5 * 128], BF16)}
    tw_h = {"p": din("tw_p", [128, 2 * 64], F32), "u": din("tw_u", [128, 2 * 32], F32)}
    mk_h = {"p": din("mk_p", [128, 2 * 64], F32), "u": din("mk_u", [128, 2 * 32], F32)}
    y_h = nc.dram_tensor("y", [NTOK, D], F32, kind="ExternalOutput")

    wgus_h = [dint(f"wgu{f}s", [NU, 128, 4096], BF16) for f in (1, 2)]
    wds_h = [dint(f"wd{f}s", [2 * NU, 128, 1024], BF16) for f in (1, 2)]
    wins_h = dint("wins", [4, 128, 4096], BF16)
    wouts_h = dint("wouts", [2, 128, 4096], BF16)
    x1s_h = dint("x1s", [NTOK, D], F32)
    X1d_h = {"p": dint("X1d_p", [128 * 64, 1024], BF16), "u": dint("X1d_u", [128 * 32, 1024], BF16)}
    PAD = 128
    cvs_h = {"p": dint("cvs_p", [8192 + 2 * PAD, 512], BF16), "u": dint("cvs_u", [4096 + 2 * PAD, 512], BF16)}
    Bs_h = {"p": dint("Bs_p", [8192, 512], BF16), "u": dint("Bs_u", [4096, 512], BF16)}

    if dbg:
        dbg_yn = nc.dram_tensor("dbg_yn", [64, 128, 1024], BF16, kind="ExternalOutput")
        dbg_x2 = nc.dram_tensor("dbg_x2", [64, 128, 1024], F32, kind="ExternalOutput")
        dbg_x3 = nc.dram_tensor("dbg_x3", [64, 128, 1024], F32, kind="ExternalOutput")
    if schedule is None:
        schedule = ([("P1", 0, b) for b in range(16)] + [("P1", 1, b) for b in range(8)]
                    + [("P2", 0, b) for b in range(16)] + [("P2", 1, b) for b in range(8)])

    P = Prog(nc)
    P.exclusive = {"pg0", "pg1", "pu0", "pu1", "po0", "po1", "pm", "pt"}
    es = ExitStack()
    sb = lambda n, s, d: es.enter_context(nc.sbuf_tensor(n, s, d))
    psum = lambda n, s, d: es.enter_context(nc.psum_tensor(n, s, d))

    ident = sb("ident_sb", [128, 128], BF16)
    csb = sb("csb_sb", [128, 256], BF16)
    dft_sb = {g: sb(f"dft_{g}_sb", [128, 5, 128], BF16) for g in ("p", "u")}
    tw_sb = {"p": sb("tw_p_sb", [128, 2, 64], F32), "u": sb("tw_u_sb", [128, 2, 32], F32)}
    mk_sb = {"p": sb("mk_p_sb", [128, 2, 64], F32), "u": sb("mk_u_sb", [128, 2, 32], F32)}
    gtab = sb("gtab_sb", [128, 4, 1024], F32)
    gfc = sb("gfc_sb", [128, 2, 512], F32)
    cw = sb("cw_sb", [128, 3, 512], F32)
    nhalf = sb("nhalf", [128, 1], F32)
    NCOL = 32
    ssq = sb("ssq", [128, NCOL], F32)
    tt = sb("tt", [128, NCOL], F32)
    rstd = sb("rstd", [128, NCOL], F32)
    NBIG, NDN = 4, 5
    wbig = [sb(f"wbig{i}", [128, 4096], BF16) for i in range(NBIG)]
    wdn = [sb(f"wdn{i}", [128, 1024], BF16) for i in range(NDN)]
    NXT = nxt
    xt = [sb(f"xt{i}", [128, 1024], F32) for i in range(NXT)]
    junk2 = [sb(f"junk{i}", [128, 1024], BF16) for i in range(2)]
    xn = [sb(f"xn{i}", [128, 1024], BF16) for i in range(2)]
    hT = {"a": sb("hTa", [128, 8, 512], BF16), "b": sb("hTb", [128, 8, 512], BF16)}
    act = sb("act", [128, NJ, 512], BF16)
    sg = [sb(f"sg{i}", [128, 512], F32) for i in range(2)]
    NG = ng
    g2k = [sb(f"g2k{i}", [128, 1024], BF16) for i in range(NG)]

    pg = [psum(f"pg{i}", [128, 512], F32) for i in range(2)]
    pu = [psum(f"pu{i}", [128, 512], F32) for i in range(2)]
    po = [psum(f"po{i}", [128, 512], F32) for i in range(2)]
    pm = psum("pm", [128, 512], F32)
    pt = psum("pt", [128, 1024], BF16)
    acc4 = [(pg[0], "pg0"), (pg[1], "pg1"), (pu[0], "pu0"), (pu[1], "pu1")]

    def rows(h, row0, rstride, nrows, rowlen, ncols=None, col0=0):
        ncols = rowlen if ncols is None else ncols
        return bass.AP(h, row0 * rowlen + col0, [[rstride * rowlen, nrows], [1, ncols]])

    def ld(dst_ap, src_ap, key):
        P.add("sp", lambda e: e.dma_start(out=dst_ap, in_=src_ap), writes=[key], dma=True)

    ld(ident[:], ident_h.ap(), "ident")
    ld(csb[:], csb_h.ap(), "csb")
    for g in ("p", "u"):
        ld(dft_sb[g][:].rearrange("p a b -> p (a b)"), dft_h[g].ap(), ("dft", g))
        ld(tw_sb[g][:].rearrange("p a b -> p (a b)"), tw_h[g].ap(), ("tw", g))
        ld(mk_sb[g][:].rearrange("p a b -> p (a b)"), mk_h[g].ap(), ("mk", g))
    ld(gtab[:].rearrange("p a b -> p (a b)"), gtab_h.ap(), "gtab")
    ld(gfc[:].rearrange("p a b -> p (a b)"), gfc_h.ap(), "gfc")
    ld(cw[:].rearrange("p a b -> p (a b)"), cw_h.ap(), "cw")
    P.add("pool", lambda e: e.memset(nhalf[:], -0.5), writes=["nhalf"])
    P.add("dve", lambda e: e.memset(g2k[0][:], 0.0), writes=[("g2k", 0)])
    for g, _, T in (GROUPS if "pad" in pro else ()):
        G = 128 * T
        for r in (0, G + PAD):
            P.add("sp", lambda e, g=g, r=r: e.dma_start(out=rows(cvs_h[g], r, 1, 128, 512), in_=g2k[0][:, 0:512]),
                  reads=[("g2k", 0)], writes=[("cvpad", g, r)], dma=True)

    def cast(dst_h, src_h, u, rowlen, key):
        if "cast" not in pro:
            return
        n = 128 * rowlen // 2048
        P.add("pool", lambda e: e.dma_start(
            out=bass.AP(dst_h, u * 128 * rowlen, [[2048, n], [1, 2048]]),
            in_=bass.AP(src_h, u * 128 * rowlen, [[2048, n], [1, 2048]])), writes=[key], dma=True)

    for u in range(NU):
        cast(wgus_h[0], wgu_h[0], u, 4096, ("wgus", 0, u))
    for u in range(2 * NU):
        cast(wds_h[0], wd_h[0], u, 1024, ("wds", 0, u))
    for i in range(4):
        cast(wins_h, winr_h, i, 4096, ("wins", i))
    late_casts = []
    for u in range(2):
        late_casts.append((wouts_h, wout_h, u, 4096, ("wouts", u)))
    for u in range(NU):
        late_casts.append((wgus_h[1], wgu_h[1], u, 4096, ("wgus", 1, u)))
        late_casts.append((wds_h[1], wd_h[1], u, 1024, ("wds", 1, u)))
        late_casts.append((wds_h[1], wd_h[1], NU + u, 1024, ("wds", 1, NU + u)))

    realP = P

    def emit_all(P, big, dn):
        col = [0]
        xn_ctr = [0]
        g2_ctr = [0]

        def stats(src_ap, n, srckeys):
            c = col[0] % NCOL
            jq = col[0] % 2
            col[0] += 1
            P.add("act", lambda e: e.activation(out=junk2[jq][:, 0:n], in_=src_ap, func=AF.Square, accum_out=ssq[:, c:c + 1]),
                  reads=srckeys, writes=[("ssq", c), ("junk", jq)])
            P.add("pool", lambda e: e.tensor_scalar(out=tt[:, c:c + 1], in0=ssq[:, c:c + 1], scalar1=1.0 / n, scalar2=EPS,
                                                    op0=ALU.mult, op1=ALU.add), reads=[("ssq", c)], writes=[("tt", c)])
            P.add("pool", lambda e: e.tensor_tensor(out=rstd[:, c:c + 1], in0=tt[:, c:c + 1], in1=nhalf[:], op=ALU.pow),
                  reads=[("tt", c), "nhalf"], writes=[("rstd", c)])
            return rstd[:, c:c + 1], ("rstd", c)

        def norm_steps(xs_list, gi_, hkey):
            steps = []
            for i in range(4):
                xs = xs_list[i]
                st = {}

                def pre(xs=xs, st=st):
                    xkeys = [("xt", xs, 0), ("xt", xs, 1)]
                    r_ap, rkey = stats(xt[xs][:], 1024, xkeys)
                    q = xn_ctr[0] % 2
                    xn_ctr[0] += 1
                    st["q"] = q
                    P.add("dve", lambda e: e.scalar_tensor_tensor(out=xn[q][:], in0=xt[xs][:], scalar=r_ap, in1=gtab[:, gi_, :],
                                                                  op0=ALU.mult, op1=ALU.mult),
                          reads=xkeys + [rkey, "gtab"], writes=[("xn", q)])

                def pe(i=i, st=st):
                    q = st["q"]
                    for k in range(8):
                        P.add("pe", lambda e, k=k: e.transpose(out=pt[:, k * 128:(k + 1) * 128], in_=xn[q][:, k * 128:(k + 1) * 128],
                                                               identity=ident[:]), reads=[("xn", q), "ident"], writes=["pt"])
                    P.add("act", lambda e: e.activation(out=hT[hkey][:, :, i * 128:(i + 1) * 128],
                                                        in_=pt[:].rearrange("p (k t) -> p k t", k=8), func=AF.Copy),
                          reads=["pt"], writes=[("hT", hkey, i)])
                steps.append((pre, pe))
            return steps

        ptf_t = pt[:].bitcast(F32)

        def ffn(f, xs_list, gu_slots, dn_slots):
            h_t = hT["a"]
            hkeys = [("hT", "a", i) for i in range(4)]
            for u in range(NU):
                ws, wkey = big.next(wgus_h[f].ap()[u], ("wgus", f, u))
                wv = ws[:].rearrange("p (a b c d) -> p a b c d", a=2, b=2, c=8)
                for jj in range(2):
                    j = 2 * u + jj
                    par = j % 2
                    for gu, bank, bkey in ((0, pg[par], f"pg{par}"), (1, pu[par], f"pu{par}")):
                        for k in range(8):
                            P.add("pe", lambda e, bank=bank, jj=jj, gu=gu, k=k, wv=wv: e.matmul(
                                bank[:], lhsT=wv[:, jj, gu, k, :], rhs=h_t[:, k, :], start=(k == 0), stop=(k == 7)),
                                reads=[wkey] + hkeys, writes=[bkey])
                    P.add("act", lambda e, par=par: e.activation(out=sg[par][:], in_=pg[par][:], func=AF.Silu),
                          reads=[f"pg{par}"], writes=[("sg", par)])
                    P.add("dve", lambda e, par=par, j=j: e.tensor_tensor(out=act[:, j, :], in0=pu[par][:], in1=sg[par][:],
                                                                          op=ALU.mult),
                          reads=[f"pu{par}", ("sg", par)], writes=[("act", j)])
                for fn in gu_slots.get(u, ()):
                    fn()
            slot = 0
            for h in range(2):
                banks = acc4 if h == 0 else [(po[0], "po0"), (po[1], "po1"), (pm, "pm"), (ptf_t, "pt")]
                for u in range(NU):
                    ws, wkey = dn.next(wds_h[f].ap()[h * NU + u], ("wds", f, h * NU + u))
                    wv = ws[:].rearrange("p (a b) -> p a b", a=2)
                    for jj in range(2):
                        j = 2 * u + jj
                        for i in range(4):
                            bank, bkey = banks[i]
                            P.add("pe", lambda e, bank=bank, j=j, i=i, jj=jj, wv=wv: e.matmul(
                                bank[:, :], lhsT=act[:, j, i * 128:(i + 1) * 128], rhs=wv[:, jj, :],
                                start=(j == 0), stop=(j == NJ - 1)),
                                reads=[wkey, ("act", j)], writes=[bkey])
                    if u % 2 == 1:
                        for fn in dn_slots.get(slot, ()):
                            fn()
                        slot += 1
                for i in range(4):
                    bank, bkey = banks[i]
                    xs = xs_list[i]
                    P.add("dve", lambda e, bank=bank, xs=xs, h=h: e.scalar_tensor_tensor(
                        out=xt[xs][:, h * 512:(h + 1) * 512], in0=bank[:, :], scalar=0.5, in1=xt[xs][:, h * 512:(h + 1) * 512],
                        op0=ALU.mult, op1=ALU.add),
                        reads=[bkey, ("xt", xs, h)], writes=[("xt", xs, h)])

        def alloc_g2(n):
            sidx = [(g2_ctr[0] + i) % NG for i in range(n)]
            g2_ctr[0] += n
            return sidx

        ptf = pt[:].bitcast(F32)

        def p1_block(gi, b, k):
            g, g0, T = GROUPS[gi]
            xs_list = [(4 * (k % 2) + i) for i in range(4)]
            blk = {"kind": "P1", "xs": xs_list, "f": 0}

            def loads():
                for i in range(4):
                    t2 = 4 * b + i
                    xs = xs_list[i]
                    P.add("sp", lambda e, xs=xs, t2=t2: e.dma_start(out=xt[xs][:], in_=rows(x_h, g0 + t2, T, 128, 1024)),
                          writes=[("xt", xs, 0), ("xt", xs, 1)], dma=True)
            blk["loads"] = loads
            blk["front_T"] = lambda: norm_steps(xs_list, 0, "a")
            blk["front_gu"] = {}

            def after_dn():
                for i in range(4):
                    t2 = 4 * b + i
                    xs = xs_list[i]
                    P.add("sp", lambda e, xs=xs, t2=t2: e.dma_start(out=rows(x1s_h, g0 + t2, T, 128, 1024), in_=xt[xs][:]),
                          reads=[("xt", xs, 0), ("xt", xs, 1)], writes=[("x1s", g, t2)], dma=True)
            blk["after_dn"] = after_dn

            def tail():
                nb = norm_steps(xs_list, 1, "b")
                st = {}

                def setup():
                    gs = alloc_g2(14)
                    st["Zt"] = [g2k[gs[i]] for i in range(4)]
                    st["Zk"] = [("g2k", gs[i]) for i in range(4)]
                    st["BC"] = [g2k[gs[4 + i]] for i in range(4)]
                    st["BCk"] = [("g2k", gs[4 + i]) for i in range(4)]
                    st["Vt"] = [g2k[gs[8 + i]][:].bitcast(F32) for i in range(4)]
                    st["Vk"] = [("g2k", gs[8 + i]) for i in range(4)]
                    st["X1o"] = [g2k[gs[10]], g2k[gs[11]]]
                    st["uf"] = [g2k[gs[12 + h // 2]][:, (h % 2) * 512:(h % 2 + 1) * 512] for h in range(4)]
                    st["ufk"] = [("g2k", gs[12 + h // 2]) for h in range(4)]
                    st["rot"] = 0

                def win_f():
                    if "Zt" not in st:
                        setup()
                    ws, wkey = big.next(wins_h.ap()[0], ("wins", 0))
                    wv = ws[:].rearrange("p (k c) -> p k c", k=8)
                    for h in range(4):
                        bank, bkey = ((po[0], "po0"), (po[1], "po1"))[st["rot"] % 2]
                        st["rot"] += 1
                        for kc in range(8):
                            P.add("pe", lambda e, bank=bank, kc=kc, h=h, wv=wv: e.matmul(
                                bank[:], lhsT=wv[:, kc, h * 128:(h + 1) * 128], rhs=hT["b"][:, kc, :], start=(kc == 0), stop=(kc == 7)),
                                reads=[wkey] + [("hT", "b", i) for i in range(4)], writes=[bkey])
                        uf, ufk = st["uf"][h], st["ufk"][h]
                        if h % 2 == 0:
                            P.add("act", lambda e, bank=bank, uf=uf: e.activation(out=uf, in_=bank[:], func=AF.Copy),
                                  reads=[bkey], writes=[ufk], group=("uf", gi, b, h // 2))
                        else:
                            P.add("dve", lambda e, bank=bank, uf=uf: e.tensor_copy(out=uf, in_=bank[:]),
                                  reads=[bkey], writes=[ufk], group=("uf", gi, b, h // 2))

                def chan(i):
                    Zt, Zk = st["Zt"], st["Zk"]
                    if i % 2 == 0:
                        bks = ((po[0][:], "po0"), (po[1][:], "po1"))
                    else:
                        bks = ((pm[:], "pm"), (ptf, "pt"))
                    for h in range(4):
                        bank, bkey = bks[h // 2]
                        P.add("pe", lambda e, bank=bank, h=h: e.matmul(bank[:, (h % 2) * 256:(h % 2 + 1) * 256],
                                                                         lhsT=st["uf"][h][:, i * 128:(i + 1) * 128], rhs=csb[:],
                                                                         start=True, stop=True),
                              reads=[st["ufk"][h], "csb"], writes=[bkey])
                    for hh in range(2):
                        bank, bkey = bks[hh]
                        bv = bank.rearrange("p (h r c) -> p h r c", h=2, r=2)
                        zre = Zt[i][:, hh * 256:(hh + 1) * 256].rearrange("p (h c) -> p h c", h=2)
                        zim = Zt[i][:, 512 + hh * 256:512 + (hh + 1) * 256].rearrange("p (h c) -> p h c", h=2)
                        P.add("act", lambda e, bv=bv, zre=zre: e.activation(out=zre, in_=bv[:, :, 0, :], func=AF.Copy),
                              reads=[bkey], writes=[Zk[i]], group=("Z", gi, b, i))
                        P.add("dve", lambda e, bv=bv, zim=zim: e.tensor_copy(out=zim, in_=bv[:, :, 1, :]),
                              reads=[bkey], writes=[Zk[i]], group=("Z", gi, b, i))

                def win(cb):
                    if "Zt" not in st:
                        setup()
                    Zt, Zk, BC, BCk, Vt, Vk = st["Zt"], st["Zk"], st["BC"], st["BCk"], st["Vt"], st["Vk"]
                    ws, wkey = big.next(wins_h.ap()[cb], ("wins", cb))
                    wv = ws[:].rearrange("p (k c) -> p k c", k=8)
                    for i in range(4):
                        bank, bkey = ((po[0], "po0"), (po[1], "po1"))[st["rot"] % 2]
                        st["rot"] += 1
                        for kc in range(8):
                            P.add("pe", lambda e, bank=bank, kc=kc, i=i, wv=wv: e.matmul(
                                bank[:], lhsT=hT["b"][:, kc, i * 128:(i + 1) * 128], rhs=wv[:, kc, :], start=(kc == 0), stop=(kc == 7)),
                                reads=[wkey, ("hT", "b", i)], writes=[bkey])
                        if cb == 1:
                            P.add("act", lambda e, bank=bank, i=i: e.activation(out=BC[i][:, 0:512], in_=bank[:], func=AF.Copy),
                                  reads=[bkey], writes=[BCk[i]], group=("BC", gi, b, i))
                        elif cb == 3:
                            P.add("act", lambda e, bank=bank, i=i: e.activation(out=Vt[i], in_=bank[:], func=AF.Copy),
                                  reads=[bkey], writes=[Vk[i]])
                        else:
                            P.add("dve", lambda e, bank=bank, i=i: e.tensor_tensor(out=BC[i][:, 512:1024], in0=bank[:], in1=Vt[i],
                                                                                    op=ALU.mult),
                                  reads=[bkey, Vk[i]], writes=[BCk[i]], group=("BC", gi, b, i))
                    if cb == 2:
                        for i in range(4):
                            t2 = 4 * b + i
                            P.add("sp", lambda e, i=i, t2=t2: e.dma_start(out=rows(Bs_h[g], t2, T, 128, 512), in_=BC[i][:, 0:512]),
                                  reads=[BCk[i]], writes=[("Bs", g, t2)], dma=True)
                            P.add("sp", lambda e, i=i, t2=t2: e.dma_start(out=rows(cvs_h[g], PAD + t2, T, 128, 512),
                                                                             in_=BC[i][:, 512:1024]),
                                  reads=[BCk[i]], writes=[("cvs", g, t2)], dma=True)

                def s1(i):
                    Zt, Zk, Vt, Vk = st["Zt"], st["Zk"], st["Vt"], st["Vk"]
                    dft = dft_sb[g]
                    t2 = 4 * b + i
                    if i % 2 == 0:
                        br, brk, bi, bik = po[0][:], "po0", po[1][:], "po1"
                    else:
                        br, brk, bi, bik = pm[:], "pm", ptf, "pt"
                    tmpa, tmpak = Vt[i % 2], Vk[i % 2]
                    xo, xok = st["X1o"][i % 2], Vk[2 + i % 2]
                    zr, zi = Zt[i][:, 0:512], Zt[i][:, 512:1024]
                    zkeys = [Zk[i], ("dft", g)]
                    P.add("pe", lambda e: e.matmul(br, lhsT=dft[:, 0, :], rhs=zr, start=True, stop=False), reads=zkeys, writes=[brk])
                    P.add("pe", lambda e: e.matmul(br, lhsT=dft[:, 2, :], rhs=zi, start=False, stop=True), reads=zkeys, writes=[brk])
                    P.add("pe", lambda e: e.matmul(bi, lhsT=dft[:, 0, :], rhs=zi, start=True, stop=False), reads=zkeys, writes=[bik])
                    P.add("pe", lambda e: e.matmul(bi, lhsT=dft[:, 1, :], rhs=zr, start=False, stop=True), reads=zkeys, writes=[bik])
                    twr = tw_sb[g][:, 0, t2:t2 + 1]
                    twi = tw_sb[g][:, 1, t2:t2 + 1]
                    P.add("act", lambda e: e.activation(out=tmpa, in_=bi, func=AF.Copy, scale=twi),
                          reads=[bik, ("tw", g)], writes=[tmpak])
                    P.add("dve", lambda e: e.scalar_tensor_tensor(out=xo[:, 0:512], in0=br, scalar=twr, in1=tmpa,
                                                                  op0=ALU.mult, op1=ALU.subtract),
                          reads=[brk, ("tw", g), tmpak], writes=[xok], group=("x1o", gi, t2))
                    P.add("act", lambda e: e.activation(out=tmpa, in_=br, func=AF.Copy, scale=twi),
                          reads=[brk, ("tw", g), xok], writes=[tmpak])
                    P.add("dve", lambda e: e.scalar_tensor_tensor(out=xo[:, 512:1024], in0=bi, scalar=twr, in1=tmpa,
                                                                  op0=ALU.mult, op1=ALU.add),
                          reads=[bik, ("tw", g), tmpak], writes=[xok])
                    P.add("sp", lambda e: e.dma_start(out=rows(X1d_h[g], t2, T, 128, 1024), in_=xo[:]),
                          reads=[xok], writes=[("X1d", g, t2)], dma=True)

                pre_now = []
                slots = {
                    0: [nb[0][0], nb[1][0]],
                    1: [nb[0][1], nb[2][0]],
                    2: [nb[1][1], nb[3][0]],
                    3: [nb[2][1]],
                    4: [nb[3][1], win_f],
                    5: [lambda: win(1), lambda: chan(0), lambda: chan(1)],
                    6: [lambda: win(3), lambda: chan(2), lambda: chan(3)],
                    7: [lambda: win(2)],
                    8: [lambda: s1(0), lambda: s1(1)],
                    9: [lambda: s1(2), lambda: s1(3)],
                }
                return pre_now, slots
            blk["tail"] = tail
            return blk

        def p2_block(gi, b, k):
            g, g0, T = GROUPS[gi]
            J = 128 // T
            xs_list = [(4 * (k % 2) + i) for i in range(4)]
            blk = {"kind": "P2", "xs": xs_list, "f": 1}
            st = {}
            all_t2 = [("X1d", g, t) for t in range(T)]
            all_cv = [("cvs", g, t) for t in range(T)] + [("cvpad", g, 0), ("cvpad", g, 128 * T + PAD)]
            all_B = [("Bs", g, t) for t in range(T)]
            all_x1 = [("x1s", g, t) for t in range(T)]
            dft = dft_sb[g]

            def setup():
                gs = alloc_g2(15)
                st["gs"] = gs

            def tiles(i):
                gs = st["gs"]
                q = i % 2
                return dict(xin=g2k[gs[q]], xink=("g2k", gs[q]),
                            c0=g2k[gs[2 + 2 * q]], c0k=("g2k", gs[2 + 2 * q]),
                            c1=g2k[gs[3 + 2 * q]], c1k=("g2k", gs[3 + 2 * q]),
                            ctmp=[g2k[gs[6 + j]][:].bitcast(F32) for j in range(3)],
                            ctk=[("g2k", gs[6 + j]) for j in range(3)],
                            yn=g2k[gs[9 + q]], ynk=("g2k", gs[9 + q]),
                            yT=g2k[gs[11 + i]], yTk=("g2k", gs[11 + i]))

            def loads_x1():
                for i in range(4):
                    kk = 4 * b + i
                    xs = xs_list[i]
                    for js in range(J):
                        ps_ = slice(js * T, (js + 1) * T)
                        P.add("sp", lambda e, ps_=ps_, js=js, xs=xs, kk=kk: e.dma_start(
                            out=xt[xs][ps_, :], in_=rows(x1s_h, g0 + kk + T * js, 128, T, 1024)),
                            reads=all_x1, writes=[("xt", xs, 0), ("xt", xs, 1)], dma=True, group=("x1ld", gi, b, i))

            def loads_tile(i):
                if "gs" not in st:
                    setup()
                t = tiles(i)
                kk = 4 * b + i
                tag = (gi, b, i)
                for js in range(J):
                    ps_ = slice(js * T, (js + 1) * T)
                    P.add("sp", lambda e, ps_=ps_, js=js: e.dma_start(
                        out=t["xin"][ps_, :], in_=rows(X1d_h[g], (kk + T * js) * T, 1, T, 1024)),
                        reads=all_t2, writes=[t["xink"]], dma=True, group=("x1in",) + tag)
                    P.add("sp", lambda e, ps_=ps_, js=js: e.dma_start(
                        out=t["c0"][ps_, :], in_=bass.AP(cvs_h[g], (PAD - 1 + kk + T * js) * 512, [[128 * 512, T], [1, 1024]])),
                        reads=all_cv, writes=[t["c0k"]], dma=True, group=("c0",) + tag)
                    P.add("sp", lambda e, ps_=ps_, js=js: e.dma_start(
                        out=t["c1"][ps_, 0:512], in_=bass.AP(cvs_h[g], (PAD + 1 + kk + T * js) * 512, [[128 * 512, T], [1, 512]])),
                        reads=all_cv, writes=[t["c1k"]], dma=True, group=("c1",) + tag)
                    P.add("sp", lambda e, ps_=ps_, js=js: e.dma_start(
                        out=t["c1"][ps_, 512:1024], in_=rows(Bs_h[g], kk + T * js, 128, T, 512)),
                        reads=all_B, writes=[t["c1k"]], dma=True, group=("c1",) + tag)

            def conv(i):
                t = tiles(i)
                kk = 4 * b + i
                c0, c0k, c1, c1k = t["c0"], t["c0k"], t["c1"], t["c1k"]
                ctmp, ctk = t["ctmp"], t["ctk"]
                mp_ = mk_sb[g][:, 0, kk:kk + 1]
                mn_ = mk_sb[g][:, 1, kk:kk + 1]
                P.add("dve", lambda e: e.scalar_tensor_tensor(out=ctmp[0], in0=c0[:, 0:512], scalar=mp_, in1=cw[:, 0, :],
                                                              op0=ALU.mult, op1=ALU.mult),
                      reads=[c0k, ("mk", g), "cw"], writes=[ctk[0]])
                P.add("pool", lambda e: e.tensor_tensor(out=ctmp[1], in0=c0[:, 512:1024], in1=cw[:, 1, :], op=ALU.mult),
                      reads=[c0k, "cw"], writes=[ctk[1]])
                P.add("dve", lambda e: e.scalar_tensor_tensor(out=ctmp[2], in0=c1[:, 0:512], scalar=mn_, in1=cw[:, 2, :],
                                                              op0=ALU.mult, op1=ALU.mult),
                      reads=[c1k, ("mk", g), "cw"], writes=[ctk[2]])
                P.add("pool", lambda e: e.tensor_tensor(out=ctmp[0], in0=ctmp[0], in1=ctmp[1], op=ALU.add),
                      reads=[ctk[0], ctk[1]], writes=[ctk[0]])
                P.add("pool", lambda e: e.tensor_tensor(out=ctmp[0], in0=ctmp[0], in1=ctmp[2], op=ALU.add),
                      reads=[ctk[0], ctk[2]], writes=[ctk[0]])
                P.add("pool", lambda e: e.tensor_tensor(out=ctmp[1], in0=ctmp[0], in1=c1[:, 512:1024], op=ALU.mult),
                      reads=[ctk[0], c1k], writes=[ctk[1]])

            def s2(i):
                t = tiles(i)
                tag = (gi, b, i)
                xin, xink = t["xin"], t["xink"]
                ctmp, ctk, ynq, ynk = t["ctmp"], t["ctk"], t["yn"], t["ynk"]
                P.add("pe", lambda e: e.matmul(pm[:], lhsT=dft[:, 3, :], rhs=xin[:, 0:512], start=True, stop=False),
                      reads=[xink, ("dft", g)], writes=["pm"])
                P.add("pe", lambda e: e.matmul(pm[:], lhsT=dft[:, 4, :], rhs=xin[:, 512:1024], start=False, stop=True),
                      reads=[xink, ("dft", g)], writes=["pm"])
                rc, rck = stats(ctmp[1], 512, [ctk[1]])
                rf, rfk = stats(pm[:], 512, ["pm"])
                P.add("dve", lambda e: e.scalar_tensor_tensor(out=ynq[:, 512:1024], in0=ctmp[1], scalar=rc, in1=gfc[:, 1, :],
                                                              op0=ALU.mult, op1=ALU.mult),
                      reads=[ctk[1], rck, "gfc"], writes=[ynk], group=("yn",) + tag)
                P.add("dve", lambda e: e.scalar_tensor_tensor(out=ynq[:, 0:512], in0=pm[:], scalar=rf, in1=gfc[:, 0, :],
                                                              op0=ALU.mult, op1=ALU.mult),
                      reads=["pm", rfk, "gfc"], writes=[ynk], group=("yn",) + tag)
                if i + 2 < 4:
                    loads_tile(i + 2)

            def ty(i):
                t = tiles(i)
                ynq, ynk, yTq, yTk = t["yn"], t["ynk"], t["yT"], t["yTk"]
                for kc in range(8):
                    P.add("pe", lambda e, kc=kc: e.transpose(out=pt[:, kc * 128:(kc + 1) * 128], in_=ynq[:, kc * 128:(kc + 1) * 128],
                                                             identity=ident[:]),
                          reads=[ynk, "ident"], writes=["pt"])
                P.add("act", lambda e: e.activation(out=yTq[:], in_=pt[:], func=AF.Copy), reads=["pt"], writes=[yTk])

            def wo_all():
                for h in range(2):
                    ws, wkey = big.next(wouts_h.ap()[h], ("wouts", h))
                    wv = ws[:].rearrange("p (k c) -> p k c", k=8)
                    for i in range(4):
                        t = tiles(i)
                        xs = xs_list[i]
                        bank, bkey = ((po[0], "po0"), (po[1], "po1"))[i % 2]
                        yv = t["yT"][:].rearrange("p (k t) -> p k t", k=8)
                        for kc in range(8):
                            P.add("pe", lambda e, bank=bank, kc=kc, yv=yv, wv=wv: e.matmul(bank[:], lhsT=yv[:, kc, :], rhs=wv[:, kc, :],
                                                                                            start=(kc == 0), stop=(kc == 7)),
                                  reads=[wkey, t["yTk"]], writes=[bkey])
                        P.add("dve", lambda e, bank=bank, xs=xs, h=h: e.tensor_tensor(out=xt[xs][:, h * 512:(h + 1) * 512], in0=bank[:],
                                                                                       in1=xt[xs][:, h * 512:(h + 1) * 512], op=ALU.add),
                              reads=[bkey, ("xt", xs, h)], writes=[("xt", xs, h)])

            def front_gu():
                n3 = norm_steps(xs_list, 2, "a")
                st["n3"] = n3
                return {
                    0: [lambda: loads_tile(0), lambda: loads_tile(1)],
                    1: [lambda: conv(0)],
                    2: [lambda: s2(0), lambda: conv(1), loads_x1],
                    3: [lambda: s2(1)],
                    4: [lambda: ty(0), lambda: conv(2)],
                    5: [lambda: ty(1), lambda: s2(2), lambda: conv(3)],
                    6: [lambda: s2(3)],
                    7: [lambda: ty(2)],
                    8: [lambda: ty(3)],
                    9: [wo_all, n3[0][0]],
                    10: [n3[1][0]],
                }
            blk["front_gu"] = front_gu
            blk["front_T"] = lambda: st["n3"]
            blk["front_T_pre_done"] = 2

            def after_dn():
                for i in range(4):
                    kk = 4 * b + i
                    xs = xs_list[i]
                    xkeys = [("xt", xs, 0), ("xt", xs, 1)]
                    r_ap, rkey = stats(xt[xs][:], 1024, xkeys)
                    P.add("dve", lambda e, xs=xs, r_ap=r_ap: e.scalar_tensor_tensor(out=xt[xs][:], in0=xt[xs][:], scalar=r_ap,
                                                                                    in1=gtab[:, 3, :], op0=ALU.mult, op1=ALU.mult),
                          reads=xkeys + [rkey, "gtab"], writes=xkeys)
                    for js in range(J):
                        ps_ = slice(js * T, (js + 1) * T)
                        P.add("sp", lambda e, ps_=ps_, js=js, xs=xs, kk=kk: e.dma_start(
                            out=rows(y_h, g0 + kk + T * js, 128, T, 1024), in_=xt[xs][ps_, :]),
                            reads=xkeys, writes=[("y", g, kk, js)], dma=True)
            blk["after_dn"] = lambda: None
            blk["post"] = after_dn
            blk["tail"] = None
            return blk

        blocks = [(p1_block if ph == "P1" else p2_block)(gi, b, k) for k, (ph, gi, b) in enumerate(schedule)]

        def run_front_inline(blk):
            if blk["kind"] == "P1":
                blk["loads"]()
                for pre, pe in blk["front_T"]():
                    pre()
                    pe()
            else:
                slots = blk["front_gu"]()
                for s in sorted(slots):
                    for fn in slots[s]:
                        fn()
                n3 = blk["front_T"]()
                n3[0][1](); n3[1][1]()
                n3[2][0](); n3[2][1](); n3[3][0](); n3[3][1]()

        nb = len(blocks)
        if nb:
            run_front_inline(blocks[0])
        tail_pending = None
        post_pending = None
        for k, blk in enumerate(blocks):
            nxt = blocks[k + 1] if k + 1 < nb else None
            gu_slots = {}
            dn_slots = {}

            def addslot(d, s, fns):
                d.setdefault(s, []).extend(fns)
            transition = tail_pending is not None and nxt is not None and nxt["kind"] == "P2"
            if nxt is not None and nxt["kind"] == "P2" and not transition:
                nxt["front_slots"] = nxt["front_gu"]()
                addslot(gu_slots, 0, nxt["front_slots"].pop(0))
            if post_pending is not None:
                addslot(gu_slots, 0, [post_pending])
                post_pending = None
            if tail_pending is not None:
                for s, fns in tail_pending.items():
                    addslot(gu_slots, s, fns)
                tail_pending = None
            if nxt is not None and not transition:
                if nxt["kind"] == "P1":
                    addslot(gu_slots, 2, [nxt["loads"]])
                    steps = nxt["front_T"]()
                    addslot(gu_slots, 8, [steps[0][0], steps[1][0]])
                    addslot(dn_slots, 0, [steps[0][1], steps[2][0]])
                    addslot(dn_slots, 1, [steps[1][1], steps[3][0]])
                    addslot(dn_slots, 2, [steps[2][1]])
                    addslot(dn_slots, 3, [steps[3][1]])
                else:
                    for s, fns in nxt["front_slots"].items():
                        addslot(gu_slots, s, fns)
                    n3 = nxt["front_T"]()
                    addslot(dn_slots, 0, [n3[0][1], n3[2][0]])
                    addslot(dn_slots, 1, [n3[1][1], n3[3][0]])
                    addslot(dn_slots, 2, [n3[2][1]])
                    addslot(dn_slots, 3, [n3[3][1]])
            if P is realP and 1 <= k:
                for _ in range(3):
                    if late_casts:
                        a_ = late_casts.pop(0)
                        addslot(gu_slots, 5, [lambda a_=a_: cast(*a_)])
            ffn(blk["f"], blk["xs"], gu_slots, dn_slots)
            blk["after_dn"]()
            if transition:
                run_front_inline(nxt)
            if blk.get("post") is not None:
                if nxt is None:
                    blk["post"]()
                else:
                    post_pending = blk["post"]
            if blk["kind"] == "P1":
                pre_now, slots = blk["tail"]()
                for fn in pre_now:
                    fn()
                if nxt is None:
                    for s in sorted(slots):
                        for fn in slots[s]:
                            fn()
                else:
                    tail_pending = slots

    class _Rec:
        def __init__(self, slot0):
            self.seq = []
            self.slot0 = slot0

        def next(self, ap, key):
            self.seq.append((ap, key))
            return self.slot0, ("dry", 0)

    dryP = Prog(nc)
    rb, rd = _Rec(wbig[0]), _Rec(wdn[0])
    emit_all(dryP, rb, rd)
    big = WStream(realP, "wbig", wbig, rb.seq)
    dn = WStream(realP, "wdn", wdn, rd.seq)
    emit_all(realP, big, dn)
    assert big.pos == len(big.seq) and dn.pos == len(dn.seq)

    P.emit(es)
    es.close()
    return nc, P


_CACHE = {}


def _prep_shared(g_ffn1, w1_gate, w1_up, w1_down, g_mix, w_in, conv_w, g_fourier, g_conv, w_out,
                 g_ffn2, w2_gate, w2_up, w2_down, g_final):
    f32 = np.float32
    A = lambda a: np.ascontiguousarray(np.asarray(a, dtype=f32))

    def gu(wg, wu):
        wg = A(wg).reshape(8, 128, NU, 2, 128).transpose(2, 1, 3, 0, 4)
        wu = A(wu).reshape(8, 128, NU, 2, 128).transpose(2, 1, 3, 0, 4)
        return np.ascontiguousarray(np.stack([wg, wu], axis=3)).reshape(NU, 128, 4096)

    def dn(wd):
        wd = A(wd).reshape(NU, 2, 128, 2, 512).transpose(3, 0, 2, 1, 4)
        return np.ascontiguousarray(wd).reshape(2 * NU, 128, 1024)

    w_in = A(w_in[0])
    m = {
        "wgu1": gu(w1_gate[0], w1_up[0]), "wgu2": gu(w2_gate[0], w2_up[0]),
        "wd1": dn(w1_down[0]), "wd2": dn(w2_down[0]),
        "winr": np.ascontiguousarray(w_in.reshape(8, 128, 4, 512).transpose(2, 1, 0, 3)).reshape(4, 128, 4096),
        "wout": np.ascontiguousarray(A(w_out[0]).reshape(8, 128, 2, 512).transpose(2, 1, 0, 3)).reshape(2, 128, 4096),
        "csb": _cs_table().astype(ml_dtypes.bfloat16),
        "gtab": np.ascontiguousarray(np.broadcast_to(
            np.concatenate([A(g_ffn1[0]), A(g_mix[0]), A(g_ffn2[0]), A(g_final)])[None, :], (128, 4096))),
        "gfc": np.ascontiguousarray(np.broadcast_to(
            np.concatenate([A(g_fourier[0]), A(g_conv[0])])[None, :], (128, 1024))),
        "cw": np.ascontiguousarray(np.broadcast_to(A(conv_w[0]).reshape(1, 1536), (128, 1536))),
        "ident": np.eye(128, dtype=np.float32).astype(ml_dtypes.bfloat16),
    }
    return m


def _core_x(c, x_prompt, x_sample):
    if c < 4:
        return np.concatenate([x_prompt[c], x_sample[c]], axis=0)
    s = 4 + 2 * (c - 4)
    return np.concatenate([x_sample[s], x_sample[s + 1], x_sample[12 + (c - 4)]], axis=0)


def kernel(x_prompt, x_sample, g_ffn1, w1_gate, w1_up, w1_down, g_mix, w_in, conv_w, g_fourier,
           g_conv, w_out, g_ffn2, w2_gate, w2_up, w2_down, g_final):
    x_prompt = np.asarray(x_prompt, dtype=np.float32)
    x_sample = np.asarray(x_sample, dtype=np.float32)
    if "nc" not in _CACHE:
        _CACHE["nc"] = build_program()[0]
        _CACHE["tabs"] = {k: _dft_tables(k) for k in "PSU"}
    nc = _CACHE["nc"]
    tabs = _CACHE["tabs"]
    shared = _prep_shared(g_ffn1, w1_gate, w1_up, w1_down, g_mix, w_in, conv_w, g_fourier, g_conv,
                          w_out, g_ffn2, w2_gate, w2_up, w2_down, g_final)
    in_maps = []
    for c in range(N_CORES):
        m = dict(shared)
        m["x"] = np.ascontiguousarray(_core_x(c, x_prompt, x_sample))
        dp, twp, mkp = tabs["P" if c < 4 else "S"]
        du, twu, mku = tabs["U"]
        m["dft_p"] = dp.reshape(128, 640)
        m["tw_p"] = twp.reshape(128, 128)
        m["mk_p"] = mkp.reshape(128, 128)
        m["dft_u"] = du.reshape(128, 640)
        m["tw_u"] = twu.reshape(128, 64)
        m["mk_u"] = mku.reshape(128, 64)
        in_maps.append(m)
    res = run_bass_kernel_spmd(nc, in_maps, core_ids=list(range(N_CORES)))
    y_prompt = np.empty_like(x_prompt)
    y_sample = np.empty_like(x_sample)
    for c in range(N_CORES):
        y = np.asarray(res.results[c]["y"], dtype=np.float32)
        if c < 4:
            y_prompt[c] = y[:8192]
            y_sample[c] = y[8192:]
        else:
            s = 4 + 2 * (c - 4)
            y_sample[s] = y[:4096]
            y_sample[s + 1] = y[4096:8192]
            y_sample[12 + (c - 4)] = y[8192:]
    return (y_prompt, y_sample)
```

```python
import numpy as np
import ml_dtypes
from contextlib import ExitStack

import concourse.bass as bass
import concourse.mybir as mybir
from concourse.bass_utils import run_bass_kernel_spmd

F32 = mybir.dt.float32
BF16 = mybir.dt.bfloat16
AF = mybir.ActivationFunctionType
ALU = mybir.AluOpType

D = 1024
DFF = 2816
NJ = DFF // 128
NU = NJ // 2
NTOK = 12288
EPS = 1e-6
N_CORES = 8


class Op:
    __slots__ = ("eng", "fn", "reads", "writes", "dma", "idx", "eidx", "signal", "sem",
                 "semval", "waits", "presem", "group")

    def __init__(self, eng, fn, reads, writes, dma):
        self.eng = eng
        self.fn = fn
        self.reads = reads
        self.writes = writes
        self.dma = dma
        self.signal = False
        self.sem = None
        self.semval = 0
        self.waits = []
        self.presem = None


class Prog:
    ENGS = ("pe", "act", "dve", "pool", "sp")
    NSEM_ENG = 4
    NSEM_DMA = 20

    def __init__(self, nc):
        self.nc = nc
        self.ops = []
        self.last_writer = {}
        self.readers = {}
        self.exclusive = set()

    def add(self, eng, fn, reads=(), writes=(), dma=False, group=None):
        ex = [r for r in reads if r in self.exclusive]
        if ex:
            writes = list(writes) + [r for r in ex if r not in writes]
        op = Op(eng, fn, tuple(reads), tuple(writes), dma)
        op.idx = len(self.ops)
        op.group = group
        self.ops.append(op)
        return op

    def _analyze(self):
        deps_of = []
        wgroup = {}
        for op in self.ops:
            deps = set()
            for r in op.reads:
                for w in self.last_writer.get(r, ()):
                    deps.add(w)
            for r in op.writes:
                gi = wgroup.get(r)
                if op.group is not None and gi is not None and gi[0] == op.group and not self.readers.get(r):
                    deps |= gi[1]
                    continue
                for w in self.last_writer.get(r, ()):
                    deps.add(w)
                for rd in self.readers.get(r, ()):
                    deps.add(rd)
            for r in op.reads:
                self.readers.setdefault(r, []).append(op.idx)
            for r in op.writes:
                gi = wgroup.get(r)
                if op.group is not None and gi is not None and gi[0] == op.group and not self.readers.get(r):
                    self.last_writer[r].append(op.idx)
                else:
                    wdeps = set(self.last_writer.get(r, ())) | set(self.readers.get(r, ()))
                    wgroup[r] = (op.group, wdeps)
                    self.last_writer[r] = [op.idx]
                    self.readers[r] = []
            deps.discard(op.idx)
            deps_of.append(deps)
        cnt = {e: 0 for e in self.ENGS}
        for op in self.ops:
            op.eidx = cnt[op.eng]
            cnt[op.eng] += 1
        water = {e: {p: -1 for p in self.ENGS} for e in self.ENGS}
        dma_waited = {e: set() for e in self.ENGS}
        for op in self.ops:
            best = {}
            for d in deps_of[op.idx]:
                p = self.ops[d]
                if p.dma:
                    if d not in dma_waited[op.eng]:
                        dma_waited[op.eng].add(d)
                        op.waits.append(d)
                        p.signal = True
                else:
                    if p.eng == op.eng and not op.dma:
                        if op.eng == "pe":
                            continue
                    if p.eidx > water[op.eng][p.eng]:
                        if p.eng not in best or self.ops[best[p.eng]].eidx < p.eidx:
                            best[p.eng] = d
            for pe_, d in best.items():
                p = self.ops[d]
                water[op.eng][pe_] = p.eidx
                op.waits.append(d)
                p.signal = True

    def _assign_sems(self, es):
        nc = self.nc
        self.eng_sems = {e: [es.enter_context(nc.semaphore(f"s_{e}{i}")) for i in range(self.NSEM_ENG)]
                         for e in ("pe", "act", "dve", "pool")}
        self.dma_sems = {e: [es.enter_context(nc.semaphore(f"d_{e}{i}")) for i in range(self.NSEM_DMA)]
                         for e in ("sp", "act", "pool")}
        ecount = {e: 0 for e in self.ENGS}
        dcount = {e: 0 for e in self.ENGS}
        dma_last = {}
        dma_val = {}
        for op in self.ops:
            if op.dma:
                k = dcount[op.eng]
                dcount[op.eng] += 1
                sem = self.dma_sems[op.eng][k % self.NSEM_DMA]
                key = (op.eng, k % self.NSEM_DMA)
                prev = dma_last.get(key)
                if prev is not None:
                    op.presem = (prev.sem, prev.semval)
                dma_val[key] = dma_val.get(key, 0) + 16
                op.sem = sem
                op.semval = dma_val[key]
                dma_last[key] = op
            elif op.signal:
                k = ecount[op.eng]
                ecount[op.eng] += 1
                op.sem = self.eng_sems[op.eng][k % self.NSEM_ENG]
                op.semval = k // self.NSEM_ENG + 1
        self.n_dma = dict(dcount)
        self.final_dma = [o for o in dma_last.values()]

    def emit(self, es):
        nc = self.nc
        self._analyze()
        self._assign_sems(es)
        block = es.enter_context(nc.Block())
        by_eng = {e: [o for o in self.ops if o.eng == e] for e in self.ENGS}
        ops = self.ops
        final_dma = self.final_dma

        def run(eng_handle, lst, is_last_waiter=False):
            for op in lst:
                if op.presem is not None:
                    eng_handle.wait_ge(op.presem[0], op.presem[1])
                for d in op.waits:
                    p = ops[d]
                    eng_handle.wait_ge(p.sem, p.semval)
                ins = op.fn(eng_handle)
                if op.dma:
                    ins.then_inc(op.sem, 16)
                elif op.signal:
                    ins.then_inc(op.sem, 1)
            if is_last_waiter:
                for o in final_dma:
                    eng_handle.wait_ge(o.sem, o.semval)

        @block.tensor
        def _(e):
            run(e, by_eng["pe"])

        @block.scalar
        def _(e):
            run(e, by_eng["act"])

        @block.vector
        def _(e):
            run(e, by_eng["dve"])

        @block.gpsimd
        def _(e):
            run(e, by_eng["pool"])

        @block.sync
        def _(e):
            run(e, by_eng["sp"], is_last_waiter=True)


def _W(n, e):
    return np.exp(-2j * np.pi * (np.asarray(e) % n) / n)


def _dft_tables(kind):
    T = 32 if kind == "U" else 64
    J = 128 // T
    A1 = np.zeros((128, 128), complex)
    tw = np.zeros((128, T), complex)
    A2 = np.zeros((128, 128), complex)
    ar = np.arange(128)
    if kind in ("P", "U"):
        S = 128 * T
        A1 = _W(128, np.outer(ar, ar)) / np.sqrt(128)
        tw = _W(S, np.outer(ar, np.arange(T)))
        F = _W(T, np.outer(np.arange(T), np.arange(T))) / np.sqrt(T)
        for js in range(J):
            A2[js * T:(js + 1) * T, js * T:(js + 1) * T] = F
    else:
        a64 = np.arange(64)
        F64 = _W(64, np.outer(a64, a64)) / 8.0
        for a in range(2):
            A1[a * 64:(a + 1) * 64, a * 64:(a + 1) * 64] = F64
            tw[a * 64:(a + 1) * 64, :] = _W(4096, np.outer(a64, a64))
        k2 = np.arange(64)
        for a in range(2):
            for jp in range(2):
                k2p = jp + 2 * (k2 % 32)
                blk = _W(64, np.outer(a64, k2p)) / 8.0 * (k2 // 32 == a)[None, :]
                A2[a * 64:(a + 1) * 64, jp * 64:(jp + 1) * 64] = blk
    G = 128 * T
    mp = np.ones(G, np.float32)
    mn = np.ones(G, np.float32)
    mp[0] = 0
    mn[G - 1] = 0
    if kind == "S":
        mp[4096] = 0
        mn[4095] = 0
    p = np.arange(128)[:, None]
    kk = np.arange(T)[None, :]
    n = kk + T * (p // T) + 128 * (p % T)
    dft = np.stack([A1.real, A1.imag, -A1.imag, A2.real, -A2.imag], axis=1)
    twt = np.stack([tw.real, tw.imag], axis=1)
    mk = np.stack([mp[n], mn[n]], axis=1)
    return (np.ascontiguousarray(dft).astype(ml_dtypes.bfloat16),
            np.ascontiguousarray(twt).astype(np.float32),
            np.ascontiguousarray(mk).astype(np.float32))


def _cs_table():
    c = np.arange(128)
    ang = 2 * np.pi * np.outer(c, c) / 128
    return np.concatenate([np.cos(ang), -np.sin(ang)], axis=1).astype(np.float32) / np.float32(np.sqrt(128))


class WStream:
    def __init__(self, P, name, slots, seq):
        self.P = P
        self.name = name
        self.slots = slots
        self.seq = seq
        self.issued = 0
        self.pos = 0

    def next(self, ap=None, key=None):
        P = self.P
        ns = len(self.slots)
        if key is not None:
            assert self.seq[self.pos][1] == key, (self.seq[self.pos][1], key)
        while self.issued < len(self.seq) and self.issued < self.pos + ns:
            i = self.issued
            src, skey = self.seq[i]
            dst = self.slots[i % ns]
            P.add("sp", lambda e, dst=dst, src=src: e.dma_start(out=dst[:], in_=src),
                  reads=[skey], writes=[(self.name, i % ns)], dma=True)
            self.issued += 1
        i = self.pos
        self.pos += 1
        return self.slots[i % ns], (self.name, i % ns)


GROUPS = (
    ("p", 0, 64),
    ("u", 8192, 32),
)


def build_program(schedule=None, dbg=False, pro=("pad", "cast", "fold"), ng=15, nxt=8):
    nc = bass.Bass("TRN2", target_bir_lowering=False)
    din = lambda n, s, d: nc.dram_tensor(n, s, d, kind="ExternalInput")
    dint = lambda n, s, d: nc.dram_tensor(n, s, d, kind=("ExternalOutput" if dbg else "Internal"))

    x_h = din("x", [NTOK, D], F32)
    wgu_h = [din(f"wgu{f}", [NU, 128, 4096], F32) for f in (1, 2)]
    wd_h = [din(f"wd{f}", [2 * NU, 128, 1024], F32) for f in (1, 2)]
    winr_h = din("winr", [4, 128, 4096], F32)
    wout_h = din("wout", [2, 128, 4096], F32)
    csb_h = din("csb", [128, 256], BF16)
    gtab_h = din("gtab", [128, 4 * 1024], F32)
    gfc_h = din("gfc", [128, 2 * 512], F32)
    cw_h = din("cw", [128, 3 * 512], F32)
    ident_h = din("ident", [128, 128], BF16)
    dft_h = {"p": din("dft_p", [128, 5 * 128], BF16), "u": din("dft_u", [128, 5 * 128], BF16)}
    tw_h = {"p": din("tw_p", [128, 2 * 64], F32), "u": din("tw_u", [128, 2 * 32], F32)}
    mk_h = {"p": din("mk_p", [128, 2 * 64], F32), "u": din("mk_u", [128, 2 * 32], F32)}
    y_h = nc.dram_tensor("y", [NTOK, D], F32, kind="ExternalOutput")

    wgus_h = [dint(f"wgu{f}s", [NU, 128, 4096], BF16) for f in (1, 2)]
    wds_h = [dint(f"wd{f}s", [2 * NU, 128, 1024], BF16) for f in (1, 2)]
    wins_h = dint("wins", [4, 128, 4096], BF16)
    wouts_h = dint("wouts", [2, 128, 4096], BF16)
    x1s_h = dint("x1s", [NTOK, D], F32)
    X1d_h = {"p": dint("X1d_p", [128 * 64, 1024], BF16), "u": dint("X1d_u", [128 * 32, 1024], BF16)}
    PAD = 128
    cvs_h = {"p": dint("cvs_p", [8192 + 2 * PAD, 512], BF16), "u": dint("cvs_u", [4096 + 2 * PAD, 512], BF16)}
    Bs_h = {"p": dint("Bs_p", [8192, 512], BF16), "u": dint("Bs_u", [4096, 512], BF16)}

    if dbg:
        dbg_yn = nc.dram_tensor("dbg_yn", [64, 128, 1024], BF16, kind="ExternalOutput")
        dbg_x2 = nc.dram_tensor("dbg_x2", [64, 128, 1024], F32, kind="ExternalOutput")
        dbg_x3 = nc.dram_tensor("dbg_x3", [64, 128, 1024], F32, kind="ExternalOutput")
    if schedule is None:
        schedule = ([("P1", 0, b) for b in range(16)] + [("P1", 1, b) for b in range(8)]
                    + [("P2", 0, b) for b in range(16)] + [("P2", 1, b) for b in range(8)])

    P = Prog(nc)
    P.exclusive = {"pg0", "pg1", "pu0", "pu1", "po0", "po1", "pm", "pt"}
    es = ExitStack()
    sb = lambda n, s, d: es.enter_context(nc.sbuf_tensor(n, s, d))
    psum = lambda n, s, d: es.enter_context(nc.psum_tensor(n, s, d))

    ident = sb("ident_sb", [128, 128], BF16)
    csb = sb("csb_sb", [128, 256], BF16)
    dft_sb = {g: sb(f"dft_{g}_sb", [128, 5, 128], BF16) for g in ("p", "u")}
    tw_sb = {"p": sb("tw_p_sb", [128, 2, 64], F32), "u": sb("tw_u_sb", [128, 2, 32], F32)}
    mk_sb = {"p": sb("mk_p_sb", [128, 2, 64], F32), "u": sb("mk_u_sb", [128, 2, 32], F32)}
    gtab = sb("gtab_sb", [128, 4, 1024], F32)
    gfc = sb("gfc_sb", [128, 2, 512], F32)
    cw = sb("cw_sb", [128, 3, 512], F32)
    nhalf = sb("nhalf", [128, 1], F32)
    NCOL = 32
    ssq = sb("ssq", [128, NCOL], F32)
    tt = sb("tt", [128, NCOL], F32)
    rstd = sb("rstd", [128, NCOL], F32)
    NBIG, NDN = 4, 5
    wbig = [sb(f"wbig{i}", [128, 4096], BF16) for i in range(NBIG)]
    wdn = [sb(f"wdn{i}", [128, 1024], BF16) for i in range(NDN)]
    NXT = nxt
    xt = [sb(f"xt{i}", [128, 1024], F32) for i in range(NXT)]
    junk2 = [sb(f"junk{i}", [128, 1024], BF16) for i in range(2)]
    xn = [sb(f"xn{i}", [128, 1024], BF16) for i in range(2)]
    hT = {"a": sb("hTa", [128, 8, 512], BF16), "b": sb("hTb", [128, 8, 512], BF16)}
    act = sb("act", [128, NJ, 512], BF16)
    sg = [sb(f"sg{i}", [128, 512], F32) for i in range(2)]
    NG = ng
    g2k = [sb(f"g2k{i}", [128, 1024], BF16) for i in range(NG)]

    pg = [psum(f"pg{i}", [128, 512], F32) for i in range(2)]
    pu = [psum(f"pu{i}", [128, 512], F32) for i in range(2)]
    po = [psum(f"po{i}", [128, 512], F32) for i in range(2)]
    pm = psum("pm", [128, 512], F32)
    pt = psum("pt", [128, 1024], BF16)
    acc4 = [(pg[0], "pg0"), (pg[1], "pg1"), (pu[0], "pu0"), (pu[1], "pu1")]

    def rows(h, row0, rstride, nrows, rowlen, ncols=None, col0=0):
        ncols = rowlen if ncols is None else ncols
        return bass.AP(h, row0 * rowlen + col0, [[rstride * rowlen, nrows], [1, ncols]])

    def ld(dst_ap, src_ap, key):
        P.add("sp", lambda e: e.dma_start(out=dst_ap, in_=src_ap), writes=[key], dma=True)

    ld(ident[:], ident_h.ap(), "ident")
    ld(csb[:], csb_h.ap(), "csb")
    for g in ("p", "u"):
        ld(dft_sb[g][:].rearrange("p a b -> p (a b)"), dft_h[g].ap(), ("dft", g))
        ld(tw_sb[g][:].rearrange("p a b -> p (a b)"), tw_h[g].ap(), ("tw", g))
        ld(mk_sb[g][:].rearrange("p a b -> p (a b)"), mk_h[g].ap(), ("mk", g))
    ld(gtab[:].rearrange("p a b -> p (a b)"), gtab_h.ap(), "gtab")
    ld(gfc[:].rearrange("p a b -> p (a b)"), gfc_h.ap(), "gfc")
    ld(cw[:].rearrange("p a b -> p (a b)"), cw_h.ap(), "cw")
    P.add("pool", lambda e: e.memset(nhalf[:], -0.5), writes=["nhalf"])
    P.add("dve", lambda e: e.memset(g2k[0][:], 0.0), writes=[("g2k", 0)])
    for g, _, T in (GROUPS if "pad" in pro else ()):
        G = 128 * T
        for r in (0, G + PAD):
            P.add("sp", lambda e, g=g, r=r: e.dma_start(out=rows(cvs_h[g], r, 1, 128, 512), in_=g2k[0][:, 0:512]),
                  reads=[("g2k", 0)], writes=[("cvpad", g, r)], dma=True)

    first = schedule[0] if schedule else None
    early_keys = []
    if first is not None and first[0] == "P1":
        _, g0_, T_ = GROUPS[first[1]]
        for i in range(4):
            t2 = 4 * first[2] + i
            P.add("sp", lambda e, i=i, t2=t2: e.dma_start(out=xt[i][:], in_=rows(x_h, g0_ + t2, T_, 128, 1024)),
                  writes=[("xt", i, 0), ("xt", i, 1)], dma=True)
            early_keys += [("xt", i, 0)]

    def cast(dst_h, src_h, u, rowlen, key):
        if "cast" not in pro:
            return
        n = 128 * rowlen // 2048
        rd = list(early_keys)
        del early_keys[:]
        P.add("pool", lambda e: e.dma_start(
            out=bass.AP(dst_h, u * 128 * rowlen, [[2048, n], [1, 2048]]),
            in_=bass.AP(src_h, u * 128 * rowlen, [[2048, n], [1, 2048]])), reads=rd, writes=[key], dma=True)

    for u in range(NU):
        cast(wgus_h[0], wgu_h[0], u, 4096, ("wgus", 0, u))
    for u in range(2 * NU):
        cast(wds_h[0], wd_h[0], u, 1024, ("wds", 0, u))
    for i in range(4):
        cast(wins_h, winr_h, i, 4096, ("wins", i))
    late_casts = []
    for u in range(2):
        late_casts.append((wouts_h, wout_h, u, 4096, ("wouts", u)))
    for u in range(NU):
        late_casts.append((wgus_h[1], wgu_h[1], u, 4096, ("wgus", 1, u)))
        late_casts.append((wds_h[1], wd_h[1], u, 1024, ("wds", 1, u)))
        late_casts.append((wds_h[1], wd_h[1], NU + u, 1024, ("wds", 1, NU + u)))

    realP = P

    def emit_all(P, big, dn):
        col = [0]
        xn_ctr = [0]
        g2_ctr = [0]

        def stats(src_ap, n, srckeys):
            c = col[0] % NCOL
            jq = col[0] % 2
            col[0] += 1
            P.add("act", lambda e: e.activation(out=junk2[jq][:, 0:n], in_=src_ap, func=AF.Square, accum_out=ssq[:, c:c + 1]),
                  reads=srckeys, writes=[("ssq", c), ("junk", jq)])
            P.add("pool", lambda e: e.tensor_scalar(out=tt[:, c:c + 1], in0=ssq[:, c:c + 1], scalar1=1.0 / n, scalar2=EPS,
                                                    op0=ALU.mult, op1=ALU.add), reads=[("ssq", c)], writes=[("tt", c)])
            P.add("pool", lambda e: e.tensor_tensor(out=rstd[:, c:c + 1], in0=tt[:, c:c + 1], in1=nhalf[:], op=ALU.pow),
                  reads=[("tt", c), "nhalf"], writes=[("rstd", c)])
            return rstd[:, c:c + 1], ("rstd", c)

        def norm_steps(xs_list, gi_, hkey):
            steps = []
            for i in range(4):
                xs = xs_list[i]
                st = {}

                def pre(xs=xs, st=st):
                    xkeys = [("xt", xs, 0), ("xt", xs, 1)]
                    r_ap, rkey = stats(xt[xs][:], 1024, xkeys)
                    q = xn_ctr[0] % 2
                    xn_ctr[0] += 1
                    st["q"] = q
                    P.add("dve", lambda e: e.scalar_tensor_tensor(out=xn[q][:], in0=xt[xs][:], scalar=r_ap, in1=gtab[:, gi_, :],
                                                                  op0=ALU.mult, op1=ALU.mult),
                          reads=xkeys + [rkey, "gtab"], writes=[("xn", q)])

                def pe(i=i, st=st):
                    q = st["q"]
                    for k in range(8):
                        P.add("pe", lambda e, k=k: e.transpose(out=pt[:, k * 128:(k + 1) * 128], in_=xn[q][:, k * 128:(k + 1) * 128],
                                                               identity=ident[:]), reads=[("xn", q), "ident"], writes=["pt"])
                    P.add("act", lambda e: e.activation(out=hT[hkey][:, :, i * 128:(i + 1) * 128],
                                                        in_=pt[:].rearrange("p (k t) -> p k t", k=8), func=AF.Copy),
                          reads=["pt"], writes=[("hT", hkey, i)])
                steps.append((pre, pe))
            return steps

        ptf_t = pt[:].bitcast(F32)

        def ffn(f, xs_list, gu_slots, dn_slots):
            h_t = hT["a"]
            hkeys = [("hT", "a", i) for i in range(4)]
            for u in range(NU):
                ws, wkey = big.next(wgus_h[f].ap()[u], ("wgus", f, u))
                wv = ws[:].rearrange("p (a b c d) -> p a b c d", a=2, b=2, c=8)
                for jj in range(2):
                    j = 2 * u + jj
                    par = j % 2
                    for gu, bank, bkey in ((0, pg[par], f"pg{par}"), (1, pu[par], f"pu{par}")):
                        for k in range(8):
                            P.add("pe", lambda e, bank=bank, jj=jj, gu=gu, k=k, wv=wv: e.matmul(
                                bank[:], lhsT=wv[:, jj, gu, k, :], rhs=h_t[:, k, :], start=(k == 0), stop=(k == 7)),
                                reads=[wkey] + hkeys, writes=[bkey])
                    P.add("act", lambda e, par=par: e.activation(out=sg[par][:], in_=pg[par][:], func=AF.Silu),
                          reads=[f"pg{par}"], writes=[("sg", par)])
                    P.add("dve", lambda e, par=par, j=j: e.tensor_tensor(out=act[:, j, :], in0=pu[par][:], in1=sg[par][:],
                                                                          op=ALU.mult),
                          reads=[f"pu{par}", ("sg", par)], writes=[("act", j)])
                for fn in gu_slots.get(u, ()):
                    fn()
            slot = 0
            for h in range(2):
                banks = acc4 if h == 0 else [(po[0], "po0"), (po[1], "po1"), (pm, "pm"), (ptf_t, "pt")]
                for u in range(NU):
                    ws, wkey = dn.next(wds_h[f].ap()[h * NU + u], ("wds", f, h * NU + u))
                    wv = ws[:].rearrange("p (a b) -> p a b", a=2)
                    for jj in range(2):
                        j = 2 * u + jj
                        for i in range(4):
                            bank, bkey = banks[i]
                            P.add("pe", lambda e, bank=bank, j=j, i=i, jj=jj, wv=wv: e.matmul(
                                bank[:, :], lhsT=act[:, j, i * 128:(i + 1) * 128], rhs=wv[:, jj, :],
                                start=(j == 0), stop=(j == NJ - 1)),
                                reads=[wkey, ("act", j)], writes=[bkey])
                    if u % 2 == 1:
                        for fn in dn_slots.get(slot, ()):
                            fn()
                        slot += 1
                for i in range(4):
                    bank, bkey = banks[i]
                    xs = xs_list[i]
                    P.add("dve", lambda e, bank=bank, xs=xs, h=h: e.scalar_tensor_tensor(
                        out=xt[xs][:, h * 512:(h + 1) * 512], in0=bank[:, :], scalar=0.5, in1=xt[xs][:, h * 512:(h + 1) * 512],
                        op0=ALU.mult, op1=ALU.add),
                        reads=[bkey, ("xt", xs, h)], writes=[("xt", xs, h)])

        def alloc_g2(n):
            sidx = [(g2_ctr[0] + i) % NG for i in range(n)]
            g2_ctr[0] += n
            return sidx

        ptf = pt[:].bitcast(F32)

        def p1_block(gi, b, k):
            g, g0, T = GROUPS[gi]
            xs_list = [(4 * (k % 2) + i) for i in range(4)]
            blk = {"kind": "P1", "xs": xs_list, "f": 0}

            def loads():
                if k == 0:
                    return
                for i in range(4):
                    t2 = 4 * b + i
                    xs = xs_list[i]
                    P.add("sp", lambda e, xs=xs, t2=t2: e.dma_start(out=xt[xs][:], in_=rows(x_h, g0 + t2, T, 128, 1024)),
                          writes=[("xt", xs, 0), ("xt", xs, 1)], dma=True)
            blk["loads"] = loads
            blk["front_T"] = lambda: norm_steps(xs_list, 0, "a")
            blk["front_gu"] = {}

            def after_dn():
                for i in range(4):
                    t2 = 4 * b + i
                    xs = xs_list[i]
                    P.add("sp", lambda e, xs=xs, t2=t2: e.dma_start(out=rows(x1s_h, g0 + t2, T, 128, 1024), in_=xt[xs][:]),
                          reads=[("xt", xs, 0), ("xt", xs, 1)], writes=[("x1s", g, t2)], dma=True)
            blk["after_dn"] = after_dn

            def tail():
                nb = norm_steps(xs_list, 1, "b")
                st = {}

                def setup():
                    gs = alloc_g2(14)
                    st["Zt"] = [g2k[gs[i]] for i in range(4)]
                    st["Zk"] = [("g2k", gs[i]) for i in range(4)]
                    st["BC"] = [g2k[gs[4 + i]] for i in range(4)]
                    st["BCk"] = [("g2k", gs[4 + i]) for i in range(4)]
                    st["Vt"] = [g2k[gs[8 + i]][:].bitcast(F32) for i in range(4)]
                    st["Vk"] = [("g2k", gs[8 + i]) for i in range(4)]
                    st["X1o"] = [g2k[gs[10]], g2k[gs[11]]]
                    st["uf"] = [g2k[gs[12 + h // 2]][:, (h % 2) * 512:(h % 2 + 1) * 512] for h in range(4)]
                    st["ufk"] = [("g2k", gs[12 + h // 2]) for h in range(4)]
                    st["rot"] = 0

                def win_f():
                    if "Zt" not in st:
                        setup()
                    ws, wkey = big.next(wins_h.ap()[0], ("wins", 0))
                    wv = ws[:].rearrange("p (k c) -> p k c", k=8)
                    for h in range(4):
                        bank, bkey = ((po[0], "po0"), (po[1], "po1"))[st["rot"] % 2]
                        st["rot"] += 1
                        for kc in range(8):
                            P.add("pe", lambda e, bank=bank, kc=kc, h=h, wv=wv: e.matmul(
                                bank[:], lhsT=wv[:, kc, h * 128:(h + 1) * 128], rhs=hT["b"][:, kc, :], start=(kc == 0), stop=(kc == 7)),
                                reads=[wkey] + [("hT", "b", i) for i in range(4)], writes=[bkey])
                        uf, ufk = st["uf"][h], st["ufk"][h]
                        if h % 2 == 0:
                            P.add("act", lambda e, bank=bank, uf=uf: e.activation(out=uf, in_=bank[:], func=AF.Copy),
                                  reads=[bkey], writes=[ufk], group=("uf", gi, b, h // 2))
                        else:
                            P.add("dve", lambda e, bank=bank, uf=uf: e.tensor_copy(out=uf, in_=bank[:]),
                                  reads=[bkey], writes=[ufk], group=("uf", gi, b, h // 2))

                def chan(i):
                    Zt, Zk = st["Zt"], st["Zk"]
                    if i % 2 == 0:
                        bks = ((po[0][:], "po0"), (po[1][:], "po1"))
                    else:
                        bks = ((pm[:], "pm"), (ptf, "pt"))
                    for h in range(4):
                        bank, bkey = bks[h // 2]
                        P.add("pe", lambda e, bank=bank, h=h: e.matmul(bank[:, (h % 2) * 256:(h % 2 + 1) * 256],
                                                                         lhsT=st["uf"][h][:, i * 128:(i + 1) * 128], rhs=csb[:],
                                                                         start=True, stop=True),
                              reads=[st["ufk"][h], "csb"], writes=[bkey])
                    for hh in range(2):
                        bank, bkey = bks[hh]
                        bv = bank.rearrange("p (h r c) -> p h r c", h=2, r=2)
                        zre = Zt[i][:, hh * 256:(hh + 1) * 256].rearrange("p (h c) -> p h c", h=2)
                        zim = Zt[i][:, 512 + hh * 256:512 + (hh + 1) * 256].rearrange("p (h c) -> p h c", h=2)
                        P.add("act", lambda e, bv=bv, zre=zre: e.activation(out=zre, in_=bv[:, :, 0, :], func=AF.Copy),
                              reads=[bkey], writes=[Zk[i]], group=("Z", gi, b, i))
                        P.add("dve", lambda e, bv=bv, zim=zim: e.tensor_copy(out=zim, in_=bv[:, :, 1, :]),
                              reads=[bkey], writes=[Zk[i]], group=("Z", gi, b, i))

                def win(cb):
                    if "Zt" not in st:
                        setup()
                    Zt, Zk, BC, BCk, Vt, Vk = st["Zt"], st["Zk"], st["BC"], st["BCk"], st["Vt"], st["Vk"]
                    ws, wkey = big.next(wins_h.ap()[cb], ("wins", cb))
                    wv = ws[:].rearrange("p (k c) -> p k c", k=8)
                    for i in range(4):
                        bank, bkey = ((po[0], "po0"), (po[1], "po1"))[st["rot"] % 2]
                        st["rot"] += 1
                        for kc in range(8):
                            P.add("pe", lambda e, bank=bank, kc=kc, i=i, wv=wv: e.matmul(
                                bank[:], lhsT=hT["b"][:, kc, i * 128:(i + 1) * 128], rhs=wv[:, kc, :], start=(kc == 0), stop=(kc == 7)),
                                reads=[wkey, ("hT", "b", i)], writes=[bkey])
                        if cb == 1:
                            P.add("act", lambda e, bank=bank, i=i: e.activation(out=BC[i][:, 0:512], in_=bank[:], func=AF.Copy),
                                  reads=[bkey], writes=[BCk[i]], group=("BC", gi, b, i))
                        elif cb == 3:
                            P.add("act", lambda e, bank=bank, i=i: e.activation(out=Vt[i], in_=bank[:], func=AF.Copy),
                                  reads=[bkey], writes=[Vk[i]])
                        else:
                            P.add("dve", lambda e, bank=bank, i=i: e.tensor_tensor(out=BC[i][:, 512:1024], in0=bank[:], in1=Vt[i],
                                                                                    op=ALU.mult),
                                  reads=[bkey, Vk[i]], writes=[BCk[i]], group=("BC", gi, b, i))
                    if cb == 2:
                        for i in range(4):
                            t2 = 4 * b + i
                            P.add("sp", lambda e, i=i, t2=t2: e.dma_start(out=rows(Bs_h[g], t2, T, 128, 512), in_=BC[i][:, 0:512]),
                                  reads=[BCk[i]], writes=[("Bs", g, t2)], dma=True)
                            P.add("sp", lambda e, i=i, t2=t2: e.dma_start(out=rows(cvs_h[g], PAD + t2, T, 128, 512),
                                                                             in_=BC[i][:, 512:1024]),
                                  reads=[BCk[i]], writes=[("cvs", g, t2)], dma=True)

                def s1(i):
                    Zt, Zk, Vt, Vk = st["Zt"], st["Zk"], st["Vt"], st["Vk"]
                    dft = dft_sb[g]
                    t2 = 4 * b + i
                    if i % 2 == 0:
                        br, brk, bi, bik = po[0][:], "po0", po[1][:], "po1"
                    else:
                        br, brk, bi, bik = pm[:], "pm", ptf, "pt"
                    tmpa, tmpak = Vt[i % 2], Vk[i % 2]
                    xo, xok = st["X1o"][i % 2], Vk[2 + i % 2]
                    zr, zi = Zt[i][:, 0:512], Zt[i][:, 512:1024]
                    zkeys = [Zk[i], ("dft", g)]
                    P.add("pe", lambda e: e.matmul(br, lhsT=dft[:, 0, :], rhs=zr, start=True, stop=False), reads=zkeys, writes=[brk])
                    P.add("pe", lambda e: e.matmul(br, lhsT=dft[:, 2, :], rhs=zi, start=False, stop=True), reads=zkeys, writes=[brk])
                    P.add("pe", lambda e: e.matmul(bi, lhsT=dft[:, 0, :], rhs=zi, start=True, stop=False), reads=zkeys, writes=[bik])
                    P.add("pe", lambda e: e.matmul(bi, lhsT=dft[:, 1, :], rhs=zr, start=False, stop=True), reads=zkeys, writes=[bik])
                    twr = tw_sb[g][:, 0, t2:t2 + 1]
                    twi = tw_sb[g][:, 1, t2:t2 + 1]
                    P.add("act", lambda e: e.activation(out=tmpa, in_=bi, func=AF.Copy, scale=twi),
                          reads=[bik, ("tw", g)], writes=[tmpak])
                    P.add("dve", lambda e: e.scalar_tensor_tensor(out=xo[:, 0:512], in0=br, scalar=twr, in1=tmpa,
                                                                  op0=ALU.mult, op1=ALU.subtract),
                          reads=[brk, ("tw", g), tmpak], writes=[xok], group=("x1o", gi, t2))
                    P.add("act", lambda e: e.activation(out=tmpa, in_=br, func=AF.Copy, scale=twi),
                          reads=[brk, ("tw", g), xok], writes=[tmpak])
                    P.add("dve", lambda e: e.scalar_tensor_tensor(out=xo[:, 512:1024], in0=bi, scalar=twr, in1=tmpa,
                                                                  op0=ALU.mult, op1=ALU.add),
                          reads=[bik, ("tw", g), tmpak], writes=[xok])
                    P.add("sp", lambda e: e.dma_start(out=rows(X1d_h[g], t2, T, 128, 1024), in_=xo[:]),
                          reads=[xok], writes=[("X1d", g, t2)], dma=True)

                pre_now = []
                slots = {
                    0: [nb[0][0], nb[1][0]],
                    1: [nb[0][1], nb[2][0]],
                    2: [nb[1][1], nb[3][0]],
                    3: [nb[2][1]],
                    4: [nb[3][1], win_f],
                    5: [lambda: win(1), lambda: chan(0), lambda: chan(1)],
                    6: [lambda: win(3), lambda: chan(2), lambda: chan(3)],
                    7: [lambda: win(2)],
                    8: [lambda: s1(0), lambda: s1(1)],
                    9: [lambda: s1(2), lambda: s1(3)],
                }
                return pre_now, slots
            blk["tail"] = tail
            return blk

        def p2_block(gi, b, k):
            g, g0, T = GROUPS[gi]
            J = 128 // T
            xs_list = [(4 * (k % 2) + i) for i in range(4)]
            blk = {"kind": "P2", "xs": xs_list, "f": 1}
            st = {}
            all_t2 = [("X1d", g, t) for t in range(T)]
            all_cv = [("cvs", g, t) for t in range(T)] + [("cvpad", g, 0), ("cvpad", g, 128 * T + PAD)]
            all_B = [("Bs", g, t) for t in range(T)]
            all_x1 = [("x1s", g, t) for t in range(T)]
            dft = dft_sb[g]

            def setup():
                gs = alloc_g2(15)
                st["gs"] = gs

            def tiles(i):
                gs = st["gs"]
                q = i % 2
                return dict(xin=g2k[gs[q]], xink=("g2k", gs[q]),
                            c0=g2k[gs[2 + 2 * q]], c0k=("g2k", gs[2 + 2 * q]),
                            c1=g2k[gs[3 + 2 * q]], c1k=("g2k", gs[3 + 2 * q]),
                            ctmp=[g2k[gs[6 + j]][:].bitcast(F32) for j in range(3)],
                            ctk=[("g2k", gs[6 + j]) for j in range(3)],
                            yn=g2k[gs[9 + q]], ynk=("g2k", gs[9 + q]),
                            yT=g2k[gs[11 + i]], yTk=("g2k", gs[11 + i]))

            def loads_x1():
                for i in range(4):
                    kk = 4 * b + i
                    xs = xs_list[i]
                    for js in range(J):
                        ps_ = slice(js * T, (js + 1) * T)
                        P.add("sp", lambda e, ps_=ps_, js=js, xs=xs, kk=kk: e.dma_start(
                            out=xt[xs][ps_, :], in_=rows(x1s_h, g0 + kk + T * js, 128, T, 1024)),
                            reads=all_x1, writes=[("xt", xs, 0), ("xt", xs, 1)], dma=True, group=("x1ld", gi, b, i))

            def loads_tile(i):
                if "gs" not in st:
                    setup()
                t = tiles(i)
                kk = 4 * b + i
                tag = (gi, b, i)
                for js in range(J):
                    ps_ = slice(js * T, (js + 1) * T)
                    P.add("sp", lambda e, ps_=ps_, js=js: e.dma_start(
                        out=t["xin"][ps_, :], in_=rows(X1d_h[g], (kk + T * js) * T, 1, T, 1024)),
                        reads=all_t2, writes=[t["xink"]], dma=True, group=("x1in",) + tag)
                    P.add("sp", lambda e, ps_=ps_, js=js: e.dma_start(
                        out=t["c0"][ps_, :], in_=bass.AP(cvs_h[g], (PAD - 1 + kk + T * js) * 512, [[128 * 512, T], [1, 1024]])),
                        reads=all_cv, writes=[t["c0k"]], dma=True, group=("c0",) + tag)
                    P.add("sp", lambda e, ps_=ps_, js=js: e.dma_start(
                        out=t["c1"][ps_, 0:512], in_=bass.AP(cvs_h[g], (PAD + 1 + kk + T * js) * 512, [[128 * 512, T], [1, 512]])),
                        reads=all_cv, writes=[t["c1k"]], dma=True, group=("c1",) + tag)
                    P.add("sp", lambda e, ps_=ps_, js=js: e.dma_start(
                        out=t["c1"][ps_, 512:1024], in_=rows(Bs_h[g], kk + T * js, 128, T, 512)),
                        reads=all_B, writes=[t["c1k"]], dma=True, group=("c1",) + tag)

            def conv(i):
                t = tiles(i)
                kk = 4 * b + i
                c0, c0k, c1, c1k = t["c0"], t["c0k"], t["c1"], t["c1k"]
                ctmp, ctk = t["ctmp"], t["ctk"]
                mp_ = mk_sb[g][:, 0, kk:kk + 1]
                mn_ = mk_sb[g][:, 1, kk:kk + 1]
                P.add("dve", lambda e: e.scalar_tensor_tensor(out=ctmp[0], in0=c0[:, 0:512], scalar=mp_, in1=cw[:, 0, :],
                                                              op0=ALU.mult, op1=ALU.mult),
                      reads=[c0k, ("mk", g), "cw"], writes=[ctk[0]])
                P.add("pool", lambda e: e.tensor_tensor(out=ctmp[1], in0=c0[:, 512:1024], in1=cw[:, 1, :], op=ALU.mult),
                      reads=[c0k, "cw"], writes=[ctk[1]])
                P.add("dve", lambda e: e.scalar_tensor_tensor(out=ctmp[2], in0=c1[:, 0:512], scalar=mn_, in1=cw[:, 2, :],
                                                              op0=ALU.mult, op1=ALU.mult),
                      reads=[c1k, ("mk", g), "cw"], writes=[ctk[2]])
                P.add("pool", lambda e: e.tensor_tensor(out=ctmp[0], in0=ctmp[0], in1=ctmp[1], op=ALU.add),
                      reads=[ctk[0], ctk[1]], writes=[ctk[0]])
                P.add("pool", lambda e: e.tensor_tensor(out=ctmp[0], in0=ctmp[0], in1=ctmp[2], op=ALU.add),
                      reads=[ctk[0], ctk[2]], writes=[ctk[0]])
                P.add("pool", lambda e: e.tensor_tensor(out=ctmp[1], in0=ctmp[0], in1=c1[:, 512:1024], op=ALU.mult),
                      reads=[ctk[0], c1k], writes=[ctk[1]])

            def s2(i):
                t = tiles(i)
                tag = (gi, b, i)
                xin, xink = t["xin"], t["xink"]
                ctmp, ctk, ynq, ynk = t["ctmp"], t["ctk"], t["yn"], t["ynk"]
                P.add("pe", lambda e: e.matmul(pm[:], lhsT=dft[:, 3, :], rhs=xin[:, 0:512], start=True, stop=False),
                      reads=[xink, ("dft", g)], writes=["pm"])
                P.add("pe", lambda e: e.matmul(pm[:], lhsT=dft[:, 4, :], rhs=xin[:, 512:1024], start=False, stop=True),
                      reads=[xink, ("dft", g)], writes=["pm"])
                rc, rck = stats(ctmp[1], 512, [ctk[1]])
                rf, rfk = stats(pm[:], 512, ["pm"])
                P.add("dve", lambda e: e.scalar_tensor_tensor(out=ynq[:, 512:1024], in0=ctmp[1], scalar=rc, in1=gfc[:, 1, :],
                                                              op0=ALU.mult, op1=ALU.mult),
                      reads=[ctk[1], rck, "gfc"], writes=[ynk], group=("yn",) + tag)
                P.add("dve", lambda e: e.scalar_tensor_tensor(out=ynq[:, 0:512], in0=pm[:], scalar=rf, in1=gfc[:, 0, :],
                                                              op0=ALU.mult, op1=ALU.mult),
                      reads=["pm", rfk, "gfc"], writes=[ynk], group=("yn",) + tag)
                if i + 2 < 4:
                    loads_tile(i + 2)

            def ty(i):
                t = tiles(i)
                ynq, ynk, yTq, yTk = t["yn"], t["ynk"], t["yT"], t["yTk"]
                for kc in range(8):
                    P.add("pe", lambda e, kc=kc: e.transpose(out=pt[:, kc * 128:(kc + 1) * 128], in_=ynq[:, kc * 128:(kc + 1) * 128],
                                                             identity=ident[:]),
                          reads=[ynk, "ident"], writes=["pt"])
                P.add("act", lambda e: e.activation(out=yTq[:], in_=pt[:], func=AF.Copy), reads=["pt"], writes=[yTk])

            def wo_all():
                for h in range(2):
                    ws, wkey = big.next(wouts_h.ap()[h], ("wouts", h))
                    wv = ws[:].rearrange("p (k c) -> p k c", k=8)
                    for i in range(4):
                        t = tiles(i)
                        xs = xs_list[i]
                        bank, bkey = ((po[0], "po0"), (po[1], "po1"))[i % 2]
                        yv = t["yT"][:].rearrange("p (k t) -> p k t", k=8)
                        for kc in range(8):
                            P.add("pe", lambda e, bank=bank, kc=kc, yv=yv, wv=wv: e.matmul(bank[:], lhsT=yv[:, kc, :], rhs=wv[:, kc, :],
                                                                                            start=(kc == 0), stop=(kc == 7)),
                                  reads=[wkey, t["yTk"]], writes=[bkey])
                        P.add("dve", lambda e, bank=bank, xs=xs, h=h: e.tensor_tensor(out=xt[xs][:, h * 512:(h + 1) * 512], in0=bank[:],
                                                                                       in1=xt[xs][:, h * 512:(h + 1) * 512], op=ALU.add),
                              reads=[bkey, ("xt", xs, h)], writes=[("xt", xs, h)])

            def front_gu():
                n3 = norm_steps(xs_list, 2, "a")
                st["n3"] = n3
                return {
                    0: [lambda: loads_tile(0), lambda: loads_tile(1)],
                    1: [lambda: conv(0)],
                    2: [lambda: s2(0), lambda: conv(1), loads_x1],
                    3: [lambda: s2(1)],
                    4: [lambda: ty(0), lambda: conv(2)],
                    5: [lambda: ty(1), lambda: s2(2), lambda: conv(3)],
                    6: [lambda: s2(3)],
                    7: [lambda: ty(2)],
                    8: [lambda: ty(3)],
                    9: [wo_all, n3[0][0]],
                    10: [n3[1][0]],
                }
            blk["front_gu"] = front_gu
            blk["front_T"] = lambda: st["n3"]
            blk["front_T_pre_done"] = 2

            def after_dn():
                for i in range(4):
                    kk = 4 * b + i
                    xs = xs_list[i]
                    xkeys = [("xt", xs, 0), ("xt", xs, 1)]
                    r_ap, rkey = stats(xt[xs][:], 1024, xkeys)
                    P.add("dve", lambda e, xs=xs, r_ap=r_ap: e.scalar_tensor_tensor(out=xt[xs][:], in0=xt[xs][:], scalar=r_ap,
                                                                                    in1=gtab[:, 3, :], op0=ALU.mult, op1=ALU.mult),
                          reads=xkeys + [rkey, "gtab"], writes=xkeys)
                    for js in range(J):
                        ps_ = slice(js * T, (js + 1) * T)
                        P.add("sp", lambda e, ps_=ps_, js=js, xs=xs, kk=kk: e.dma_start(
                            out=rows(y_h, g0 + kk + T * js, 128, T, 1024), in_=xt[xs][ps_, :]),
                            reads=xkeys, writes=[("y", g, kk, js)], dma=True)
            blk["after_dn"] = lambda: None
            blk["post"] = after_dn
            blk["tail"] = None
            return blk

        blocks = [(p1_block if ph == "P1" else p2_block)(gi, b, k) for k, (ph, gi, b) in enumerate(schedule)]

        def run_front_inline(blk):
            if blk["kind"] == "P1":
                blk["loads"]()
                for pre, pe in blk["front_T"]():
                    pre()
                    pe()
            else:
                slots = blk["front_gu"]()
                for s in sorted(slots):
                    for fn in slots[s]:
                        fn()
                n3 = blk["front_T"]()
                n3[0][1](); n3[1][1]()
                n3[2][0](); n3[2][1](); n3[3][0](); n3[3][1]()

        nb = len(blocks)
        if nb:
            run_front_inline(blocks[0])
        tail_pending = None
        post_pending = None
        for k, blk in enumerate(blocks):
            nxt = blocks[k + 1] if k + 1 < nb else None
            gu_slots = {}
            dn_slots = {}

            def addslot(d, s, fns):
                d.setdefault(s, []).extend(fns)
            transition = blk["kind"] == "P1" and nxt is not None and nxt["kind"] == "P2"
            if nxt is not None and nxt["kind"] == "P2" and not transition:
                nxt["front_slots"] = nxt["front_gu"]()
                addslot(gu_slots, 0, nxt["front_slots"].pop(0))
            if post_pending is not None:
                addslot(gu_slots, 0, [post_pending])
                post_pending = None
            if tail_pending is not None:
                for s, fns in tail_pending.items():
                    addslot(gu_slots, s, fns)
                tail_pending = None
            if nxt is not None and not transition:
                if nxt["kind"] == "P1":
                    addslot(gu_slots, 2, [nxt["loads"]])
                    steps = nxt["front_T"]()
                    addslot(gu_slots, 8, [steps[0][0], steps[1][0]])
                    addslot(dn_slots, 0, [steps[0][1], steps[2][0]])
                    addslot(dn_slots, 1, [steps[1][1], steps[3][0]])
                    addslot(dn_slots, 2, [steps[2][1]])
                    addslot(dn_slots, 3, [steps[3][1]])
                else:
                    for s, fns in nxt["front_slots"].items():
                        addslot(gu_slots, s, fns)
                    n3 = nxt["front_T"]()
                    addslot(dn_slots, 0, [n3[0][1], n3[2][0]])
                    addslot(dn_slots, 1, [n3[1][1], n3[3][0]])
                    addslot(dn_slots, 2, [n3[2][1]])
                    addslot(dn_slots, 3, [n3[3][1]])
            if P is realP and 1 <= k:
                for _ in range(3):
                    if late_casts:
                        a_ = late_casts.pop(0)
                        addslot(gu_slots, 5, [lambda a_=a_: cast(*a_)])
            ffn(blk["f"], blk["xs"], gu_slots, dn_slots)
            blk["after_dn"]()
            if blk.get("post") is not None:
                if nxt is None:
                    blk["post"]()
                else:
                    post_pending = blk["post"]
            if blk["kind"] == "P1":
                pre_now, slots = blk["tail"]()
                for fn in pre_now:
                    fn()
                if nxt is None or nxt["kind"] == "P2":
                    for s in sorted(slots):
                        for fn in slots[s]:
                            fn()
                    if nxt is not None:
                        run_front_inline(nxt)
                else:
                    tail_pending = slots

    class _Rec:
        def __init__(self, slot0):
            self.seq = []
            self.slot0 = slot0

        def next(self, ap, key):
            self.seq.append((ap, key))
            return self.slot0, ("dry", 0)

    dryP = Prog(nc)
    rb, rd = _Rec(wbig[0]), _Rec(wdn[0])
    emit_all(dryP, rb, rd)
    big = WStream(realP, "wbig", wbig, rb.seq)
    dn = WStream(realP, "wdn", wdn, rd.seq)
    emit_all(realP, big, dn)
    assert big.pos == len(big.seq) and dn.pos == len(dn.seq)

    P.emit(es)
    es.close()
    return nc, P


_CACHE = {}


def _prep_shared(g_ffn1, w1_gate, w1_up, w1_down, g_mix, w_in, conv_w, g_fourier, g_conv, w_out,
                 g_ffn2, w2_gate, w2_up, w2_down, g_final):
    f32 = np.float32
    A = lambda a: np.ascontiguousarray(np.asarray(a, dtype=f32))

    def gu(wg, wu):
        wg = A(wg).reshape(8, 128, NU, 2, 128).transpose(2, 1, 3, 0, 4)
        wu = A(wu).reshape(8, 128, NU, 2, 128).transpose(2, 1, 3, 0, 4)
        return np.ascontiguousarray(np.stack([wg, wu], axis=3)).reshape(NU, 128, 4096)

    def dn(wd):
        wd = A(wd).reshape(NU, 2, 128, 2, 512).transpose(3, 0, 2, 1, 4)
        return np.ascontiguousarray(wd).reshape(2 * NU, 128, 1024)

    w_in = A(w_in[0])
    m = {
        "wgu1": gu(w1_gate[0], w1_up[0]), "wgu2": gu(w2_gate[0], w2_up[0]),
        "wd1": dn(w1_down[0]), "wd2": dn(w2_down[0]),
        "winr": np.ascontiguousarray(w_in.reshape(8, 128, 4, 512).transpose(2, 1, 0, 3)).reshape(4, 128, 4096),
        "wout": np.ascontiguousarray(A(w_out[0]).reshape(8, 128, 2, 512).transpose(2, 1, 0, 3)).reshape(2, 128, 4096),
        "csb": _cs_table().astype(ml_dtypes.bfloat16),
        "gtab": np.ascontiguousarray(np.broadcast_to(
            np.concatenate([A(g_ffn1[0]), A(g_mix[0]), A(g_ffn2[0]), A(g_final)])[None, :], (128, 4096))),
        "gfc": np.ascontiguousarray(np.broadcast_to(
            np.concatenate([A(g_fourier[0]), A(g_conv[0])])[None, :], (128, 1024))),
        "cw": np.ascontiguousarray(np.broadcast_to(A(conv_w[0]).reshape(1, 1536), (128, 1536))),
        "ident": np.eye(128, dtype=np.float32).astype(ml_dtypes.bfloat16),
    }
    return m


def _core_x(c, x_prompt, x_sample):
    if c < 4:
        return np.concatenate([x_prompt[c], x_sample[c]], axis=0)
    s = 4 + 2 * (c - 4)
    return np.concatenate([x_sample[s], x_sample[s + 1], x_sample[12 + (c - 4)]], axis=0)


def kernel(x_prompt, x_sample, g_ffn1, w1_gate, w1_up, w1_down, g_mix, w_in, conv_w, g_fourier,
           g_conv, w_out, g_ffn2, w2_gate, w2_up, w2_down, g_final):
    x_prompt = np.asarray(x_prompt, dtype=np.float32)
    x_sample = np.asarray(x_sample, dtype=np.float32)
    if "nc" not in _CACHE:
        _CACHE["nc"] = build_program()[0]
        _CACHE["tabs"] = {k: _dft_tables(k) for k in "PSU"}
    nc = _CACHE["nc"]
    tabs = _CACHE["tabs"]
    shared = _prep_shared(g_ffn1, w1_gate, w1_up, w1_down, g_mix, w_in, conv_w, g_fourier, g_conv,
                          w_out, g_ffn2, w2_gate, w2_up, w2_down, g_final)
    in_maps = []
    for c in range(N_CORES):
        m = dict(shared)
        m["x"] = np.ascontiguousarray(_core_x(c, x_prompt, x_sample))
        dp, twp, mkp = tabs["P" if c < 4 else "S"]
        du, twu, mku = tabs["U"]
        m["dft_p"] = dp.reshape(128, 640)
        m["tw_p"] = twp.reshape(128, 128)
        m["mk_p"] = mkp.reshape(128, 128)
        m["dft_u"] = du.reshape(128, 640)
        m["tw_u"] = twu.reshape(128, 64)
        m["mk_u"] = mku.reshape(128, 64)
        in_maps.append(m)
    res = run_bass_kernel_spmd(nc, in_maps, core_ids=list(range(N_CORES)))
    y_prompt = np.empty_like(x_prompt)
    y_sample = np.empty_like(x_sample)
    for c in range(N_CORES):
        y = np.asarray(res.results[c]["y"], dtype=np.float32)
        if c < 4:
            y_prompt[c] = y[:8192]
            y_sample[c] = y[8192:]
        else:
            s = 4 + 2 * (c - 4)
            y_sample[s] = y[:4096]
            y_sample[s + 1] = y[4096:8192]
            y_sample[12 + (c - 4)] = y[8192:]
    return (y_prompt, y_sample)
```
